# Optimizing a Trainium2 kernel written in Bass

```python
import jax, jax.numpy as jnp
from jax import lax
import numpy as np

D_MODEL = 1024
BATCH = 2
SEQ = 16384
DEPTH = 2
DEC_BATCH = 2
DEC_SEQ = 8192
PAST_LEN = 128

HEAD_DIM = 64
A_Q_HEADS = 8
A_KV_HEADS = 2
B_Q_HEADS = 8
B_KV_HEADS = 2
A_WIDTH = A_Q_HEADS * HEAD_DIM
B_WIDTH = B_Q_HEADS * HEAD_DIM
MIX_WIDTH = A_WIDTH + B_WIDTH
A_KV_WIDTH = A_KV_HEADS * HEAD_DIM
B_KV_WIDTH = B_KV_HEADS * HEAD_DIM
IN_WIDTH = A_WIDTH + 2 * A_KV_WIDTH + B_WIDTH + 2 * B_KV_WIDTH
D_FF = 2816
GRID_W = 64
ROPE_THETA = 10000.0
WINDOW = 128
Q_BLOCK = 128
NORM_EPS = 1e-6
FFN_RESID = 0.5
ATTN_SCALE = HEAD_DIM ** -0.5

kernel_name = "hybrid_axial_window_macaron_encoder"


def rms_norm(x, g):
    xf = x.astype(jnp.float32)
    y = xf * lax.rsqrt(jnp.mean(xf * xf, axis=-1, keepdims=True) + NORM_EPS)
    return (y * g.astype(jnp.float32)).astype(x.dtype)


def swiglu(x, w_gate, w_up, w_down):
    return (jax.nn.silu(x @ w_gate) * (x @ w_up)) @ w_down


def axial_rope_tables(seq_len):
    n_rows = seq_len // GRID_W
    row = jnp.repeat(jnp.arange(n_rows, dtype=jnp.float32), GRID_W)
    col = jnp.tile(jnp.arange(GRID_W, dtype=jnp.float32), n_rows)
    n_freq = HEAD_DIM // 4
    inv_freq = ROPE_THETA ** (-jnp.arange(n_freq, dtype=jnp.float32) / n_freq)
    ang = jnp.concatenate([row[:, None] * inv_freq[None, :], col[:, None] * inv_freq[None, :]], axis=-1)
    return jnp.cos(ang), jnp.sin(ang)


def apply_rope(x, cos, sin):
    b, s, h, d = x.shape
    xf = x.astype(jnp.float32).reshape(b, s, h, d // 2, 2)
    x0, x1 = xf[..., 0], xf[..., 1]
    c = cos[None, :, None, :]
    sn = sin[None, :, None, :]
    out = jnp.stack([x0 * c - x1 * sn, x0 * sn + x1 * c], axis=-1)
    return out.reshape(b, s, h, d).astype(x.dtype)


def alibi_slopes():
    return 2.0 ** (-8.0 * jnp.arange(1, B_Q_HEADS + 1, dtype=jnp.float32) / B_Q_HEADS)


def global_attention(q, k, v):
    b, s, hq, d = q.shape
    hkv = k.shape[2]
    g = hq // hkv
    nblk = s // Q_BLOCK
    qb = (q * ATTN_SCALE).reshape(b, nblk, Q_BLOCK, hkv, g, d).transpose(1, 0, 2, 3, 4, 5)

    def one_block(qi):
        sc = jnp.einsum('bqhgd,bshd->bhgqs', qi, k, preferred_element_type=jnp.float32)
        p = jax.nn.softmax(sc, axis=-1)
        return jnp.einsum('bhgqs,bshd->bqhgd', p.astype(v.dtype), v)

    o = lax.map(one_block, qb)
    return o.transpose(1, 0, 2, 3, 4, 5).reshape(b, s, hq * d)


def window_attention(q, k, v, sink, slopes):
    b, s, hq, d = q.shape
    hkv = k.shape[2]
    g = hq // hkv
    nblk = s // Q_BLOCK
    pad = ((0, 0), (WINDOW, WINDOW), (0, 0), (0, 0))
    kp = jnp.pad(k, pad).reshape(b, nblk + 2, Q_BLOCK, hkv, d)
    vp = jnp.pad(v, pad).reshape(b, nblk + 2, Q_BLOCK, hkv, d)
    kband = jnp.concatenate([kp[:, :-2], kp[:, 1:-1], kp[:, 2:]], axis=2)
    vband = jnp.concatenate([vp[:, :-2], vp[:, 1:-1], vp[:, 2:]], axis=2)
    qb = (q * ATTN_SCALE).reshape(b, nblk, Q_BLOCK, hkv, g, d)
    sc = jnp.einsum('bnqhgd,bnkhd->bnhgqk', qb, kband, preferred_element_type=jnp.float32)
    blk = jnp.arange(nblk)[:, None] * Q_BLOCK
    qpos = blk + jnp.arange(Q_BLOCK)[None, :]
    kpos = blk - WINDOW + jnp.arange(3 * Q_BLOCK)[None, :]
    dist = jnp.abs(qpos[:, :, None] - kpos[:, None, :])
    valid = (dist <= WINDOW) & (kpos >= 0)[:, None, :] & (kpos < s)[:, None, :]
    m_h = slopes.reshape(hkv, g)[None, None, :, :, None, None]
    sc = sc - m_h * dist.astype(jnp.float32)[None, :, None, None, :, :]
    sc = jnp.where(valid[None, :, None, None, :, :], sc, -jnp.inf)
    sink_l = sink.astype(jnp.float32).reshape(hkv, g)[None, None, :, :, None, None]
    mx = jnp.maximum(jnp.max(sc, axis=-1, keepdims=True), sink_l)
    e = jnp.exp(sc - mx)
    p = e / (jnp.sum(e, axis=-1, keepdims=True) + jnp.exp(sink_l - mx))
    o = jnp.einsum('bnhgqk,bnkhd->bnqhgd', p.astype(v.dtype), vband)
    return o.reshape(b, s, hq * d)


def token_mixer(h, w_in, a_q_norm, a_k_norm, b_sink, a_out_norm, b_out_norm, w_out, cos, sin, slopes):
    b, s, _ = h.shape
    proj = h @ w_in
    offs = np.cumsum([A_WIDTH, A_KV_WIDTH, A_KV_WIDTH, B_WIDTH, B_KV_WIDTH]).tolist()
    qa, ka, va, qb, kb, vb = jnp.split(proj, offs, axis=-1)
    qa = qa.reshape(b, s, A_Q_HEADS, HEAD_DIM)
    ka = ka.reshape(b, s, A_KV_HEADS, HEAD_DIM)
    va = va.reshape(b, s, A_KV_HEADS, HEAD_DIM)
    qa = apply_rope(rms_norm(qa, a_q_norm), cos, sin)
    ka = apply_rope(rms_norm(ka, a_k_norm), cos, sin)
    oa = global_attention(qa, ka, va)
    qb = qb.reshape(b, s, B_Q_HEADS, HEAD_DIM)
    kb = kb.reshape(b, s, B_KV_HEADS, HEAD_DIM)
    vb = vb.reshape(b, s, B_KV_HEADS, HEAD_DIM)
    ob = window_attention(qb, kb, vb, b_sink, slopes)
    merged = jnp.concatenate([rms_norm(oa, a_out_norm), rms_norm(ob, b_out_norm)], axis=-1)
    return merged @ w_out


def encoder_trunk(x, ffn1_pre, ffn1_post, ffn1_w_gate, ffn1_w_up, ffn1_w_down,
                  mix_pre, mix_post, w_in, a_q_norm, a_k_norm, b_sink, a_out_norm, b_out_norm, w_out,
                  ffn2_pre, ffn2_post, ffn2_w_gate, ffn2_w_up, ffn2_w_down):
    seq_len = x.shape[1]
    cos, sin = axial_rope_tables(seq_len)
    slopes = alibi_slopes()
    for l in range(DEPTH):
        h = swiglu(rms_norm(x, ffn1_pre[l]), ffn1_w_gate[l], ffn1_w_up[l], ffn1_w_down[l])
        x = x + FFN_RESID * rms_norm(h, ffn1_post[l])
        h = token_mixer(rms_norm(x, mix_pre[l]), w_in[l], a_q_norm[l], a_k_norm[l], b_sink[l],
                        a_out_norm[l], b_out_norm[l], w_out[l], cos, sin, slopes)
        x = x + rms_norm(h, mix_post[l])
        h = swiglu(rms_norm(x, ffn2_pre[l]), ffn2_w_gate[l], ffn2_w_up[l], ffn2_w_down[l])
        x = x + FFN_RESID * rms_norm(h, ffn2_post[l])
    return x


def setup_inputs(seed: int = 0) -> dict:
    key = jax.random.key(seed)
    ks = jax.random.split(key, 24)
    f32 = jnp.float32

    def nrm(k, shape, scale):
        return jax.random.normal(k, shape, f32) * scale

    def gain(k, n):
        return jnp.ones((DEPTH, n), f32) + 0.05 * jax.random.normal(k, (DEPTH, n), f32)

    return {
        "x_prompt": jax.random.normal(ks[0], (BATCH, SEQ, D_MODEL), f32),
        "x_sample": jax.random.normal(ks[1], (DEC_BATCH, DEC_SEQ, D_MODEL), f32),
        "ffn1_pre": gain(ks[2], D_MODEL),
        "ffn1_post": gain(ks[3], D_MODEL),
        "ffn1_w_gate": nrm(ks[4], (DEPTH, D_MODEL, D_FF), D_MODEL ** -0.5),
        "ffn1_w_up": nrm(ks[5], (DEPTH, D_MODEL, D_FF), D_MODEL ** -0.5),
        "ffn1_w_down": nrm(ks[6], (DEPTH, D_FF, D_MODEL), D_FF ** -0.5),
        "mix_pre": gain(ks[7], D_MODEL),
        "mix_post": gain(ks[8], D_MODEL),
        "w_in": nrm(ks[9], (DEPTH, D_MODEL, IN_WIDTH), D_MODEL ** -0.5),
        "a_q_norm": gain(ks[10], HEAD_DIM),
        "a_k_norm": gain(ks[11], HEAD_DIM),
        "b_sink": nrm(ks[12], (DEPTH, B_Q_HEADS), 0.5),
        "a_out_norm": gain(ks[13], A_WIDTH),
        "b_out_norm": gain(ks[14], B_WIDTH),
        "w_out": nrm(ks[15], (DEPTH, MIX_WIDTH, D_MODEL), MIX_WIDTH ** -0.5),
        "ffn2_pre": gain(ks[16], D_MODEL),
        "ffn2_post": gain(ks[17], D_MODEL),
        "ffn2_w_gate": nrm(ks[18], (DEPTH, D_MODEL, D_FF), D_MODEL ** -0.5),
        "ffn2_w_up": nrm(ks[19], (DEPTH, D_MODEL, D_FF), D_MODEL ** -0.5),
        "ffn2_w_down": nrm(ks[20], (DEPTH, D_FF, D_MODEL), D_FF ** -0.5),
    }


def reference(x_prompt, x_sample, ffn1_pre, ffn1_post, ffn1_w_gate, ffn1_w_up, ffn1_w_down,
              mix_pre, mix_post, w_in, a_q_norm, a_k_norm, b_sink, a_out_norm, b_out_norm, w_out,
              ffn2_pre, ffn2_post, ffn2_w_gate, ffn2_w_up, ffn2_w_down):
    y_prompt = encoder_trunk(x_prompt, ffn1_pre, ffn1_post, ffn1_w_gate, ffn1_w_up, ffn1_w_down,
                             mix_pre, mix_post, w_in, a_q_norm, a_k_norm, b_sink, a_out_norm, b_out_norm, w_out,
                             ffn2_pre, ffn2_post, ffn2_w_gate, ffn2_w_up, ffn2_w_down)
    y_sample = encoder_trunk(x_sample, ffn1_pre, ffn1_post, ffn1_w_gate, ffn1_w_up, ffn1_w_down,
                             mix_pre, mix_post, w_in, a_q_norm, a_k_norm, b_sink, a_out_norm, b_out_norm, w_out,
                             ffn2_pre, ffn2_post, ffn2_w_gate, ffn2_w_up, ffn2_w_down)
    return (y_prompt, y_sample)
```

```python
import contextlib
import numpy as np
import ml_dtypes
import concourse.bass as bass
import concourse.mybir as mybir
from concourse.bass_utils import run_bass_kernel_spmd

F32 = mybir.dt.float32
BF16 = mybir.dt.bfloat16
AF = mybir.ActivationFunctionType
ALU = mybir.AluOpType

P = 128
TILE = 512
D = 1024
DFF = 2816
NF = DFF // P
KC = D // P
HD = 64
EPS = 1e-6
NSLOT = 5
SLOTW = 1408
GROUPS = [[0, 1, 2, 3], [4, 5, 6, 7]]
LW = 64
TSW = 1280


class Cfg:
    def __init__(self, seq_p, seq_s, depth):
        self.seq_p, self.seq_s, self.depth = seq_p, seq_s, depth
        self.TP, self.TS = seq_p // 4, seq_s // 4
        self.NT = self.TP + self.TS
        self.ntp, self.nts = self.TP // TILE, self.TS // TILE
        self.ntile = self.ntp + self.nts
        self.nag = (self.ntile + 2) // 3
        self.ncst = depth * LW + 8


COMPUTE = ("pe", "act", "dve", "pool")
QUEUES = {"sync": "sync", "gq": "pool"}


class Prog:
    def __init__(self):
        self.q = {e: [] for e in ("pe", "act", "dve", "pool", "sync")}
        self.cnt = {e: 0 for e in COMPUTE}
        self.waited = {e: {} for e in self.q}
        self.last_w = {}
        self.readers = {}
        self.dsem = {"sync": [0] * NSLOT, "gq": [0] * 12}
        self.drr = {"sync": 0, "gq": 0}
        self.nag = 0
        self.ninstr = 0

    def _wait(self, eng, ev):
        sem, val = ev[0], ev[1]
        if self.waited[eng].get(sem, 0) >= val:
            return
        self.waited[eng][sem] = val
        self.q[eng].append(("wait", sem, val))

    def _deps(self, eng, reads, writes, is_dma):
        evs = []
        for k in reads:
            e = self.last_w.get(k)
            if e is not None:
                evs.append((e, "raw"))
        for k in writes:
            e = self.last_w.get(k)
            if e is not None:
                evs.append((e, "waw"))
            for src, e in self.readers.get(k, {}).items():
                evs.append((e, "war"))
        for (sem, val, src), kind in evs:
            if not is_dma and src == eng:
                if eng == "pe" or kind != "raw":
                    continue
            self._wait(eng, (sem, val))

    def _commit(self, ev, reads, writes):
        for k in reads:
            d = self.readers.setdefault(k, {})
            key = ev[2] if ev[2] is not None else ev[0]
            d[key] = ev
        for k in writes:
            self.last_w[k] = ev
            self.readers[k] = {}

    def op(self, eng, fn, reads=(), writes=()):
        self._deps(eng, reads, writes, False)
        self.cnt[eng] += 1
        ev = ("s_" + eng, self.cnt[eng], eng)
        self.q[eng].append(("op", fn, "s_" + eng, 1))
        self._commit(ev, reads, writes)
        self.ninstr += 1

    def dma(self, queue, fn, reads=(), writes=(), slot=None):
        eng = QUEUES[queue]
        n = len(self.dsem[queue])
        if slot is None:
            slot = self.drr[queue]
            self.drr[queue] = (slot + 1) % n
        sem = "d_%s_%d" % (queue, slot)
        uses = self.dsem[queue][slot]
        if uses > 0:
            self._wait(eng, (sem, 16 * uses))
        self._deps(eng, reads, writes, True)
        self.dsem[queue][slot] = uses + 1
        ev = (sem, 16 * (uses + 1), None)
        self.q[eng].append(("op", fn, sem, 16))
        self._commit(ev, reads, writes)
        self.ninstr += 1

    def ag(self, fn, reads, writes):
        eng = "pool"
        self._deps(eng, reads, writes, True)
        sem = "ag_%d" % self.nag
        self.nag += 1
        ev = (sem, 1, None)
        self.q[eng].append(("ag", fn, sem, 1))
        self._commit(ev, reads, writes)

    def sem_names(self):
        names = ["s_" + e for e in COMPUTE]
        for qn, lst in self.dsem.items():
            names += ["d_%s_%d" % (qn, i) for i in range(len(lst))]
        names += ["ag_%d" % i for i in range(self.nag)]
        return names

    def finish(self):
        for qn, lst in self.dsem.items():
            for i, uses in enumerate(lst):
                if uses:
                    self._wait("pool", ("d_%s_%d" % (qn, i), 16 * uses))


def replay(items, eng, sems):
    for it in items:
        if it[0] == "wait":
            eng.wait_ge(sems[it[1]], it[2])
        elif it[0] == "op":
            ins = it[1](eng)
            ins.then_inc(sems[it[2]], it[3])
        else:
            ins = it[1](eng)
            ins.then_inc(sems[it[2]])


def build(cfg):
    nc = bass.Bass("TRN2", target_bir_lowering=False)
    T = Prog()
    L = cfg.depth
    NT = cfg.NT
    es = contextlib.ExitStack()

    def din(name, shape, dt=F32):
        return nc.dram_tensor(name, list(shape), dt, kind="ExternalInput").ap()

    def dscr(name, shape, dt):
        return nc.dram_tensor(name, list(shape), dt).ap()

    xT = din("xT", [D, NT])
    yT = nc.dram_tensor("yT", [D, NT], F32, kind="ExternalOutput").ap()
    cs_d = din("cs", [2, P, NT])
    cst_d = din("cst", [P, cfg.ncst])
    ebias_d = din("ebias", [6, P, TILE])
    swp_d = din("swp", [P, P])
    wnames = {"wg1": (NF * P, D), "wu1": (NF * P, D), "wd1": (16 * P, SLOTW), "win": (17 * P, D),
              "wout": (8 * P, D), "wg2": (NF * P, D), "wu2": (NF * P, D), "wd2": (16 * P, SLOTW)}
    wf32, wbf = {}, {}
    for nme, (r, w) in wnames.items():
        wf32[nme] = din(nme, [L * r, w])
        wbf[nme] = dscr(nme + "_bf", [L * r, w], BF16)

    xres = dscr("xres", [D, NT], F32)
    q_scr = [dscr("q_scr%d" % l, [KC * P, NT], BF16) for l in range(L)]
    kvb_k = [dscr("kvb_k%d" % l, [P, NT], BF16) for l in range(L)]
    kvb_v = [dscr("kvb_v%d" % l, [P, (NT // P) * 192], BF16) for l in range(L)]
    kin = [[nc.dram_tensor("kin%d_%d" % (l, i), [P, 3 * TSW], BF16) for i in range(cfg.nag)] for l in range(L)]
    kall = [[nc.dram_tensor("kall%d_%d" % (l, i), [4 * P, 3 * TSW], BF16) for i in range(cfg.nag)] for l in range(L)]
    kinb = [[nc.dram_tensor("kinb%d_%d" % (l, i), [P, 640], BF16) for i in range(2)] for l in range(L)]
    kallb = [[nc.dram_tensor("kallb%d_%d" % (l, i), [4 * P, 640], BF16) for i in range(2)] for l in range(L)]

    def sb(name, shape, dt):
        return es.enter_context(nc.sbuf_tensor("sb_" + name, list(shape), dt))

    x_sb = sb("x_sb", [P, KC, TILE], F32)
    yo = sb("yo", [P, KC, TILE], F32)
    xn = sb("xn", [P, KC, TILE], BF16)
    h_sb = sb("h_sb", [P, NF, TILE], BF16)
    qbuf = sb("qbuf", [P, KC, TILE], BF16)
    wsl = [sb("wsl%d" % i, [P, SLOTW], BF16) for i in range(NSLOT)]
    KaT = sb("KaT", [P, cfg.seq_p], BF16)
    VaA = sb("VaA", [P, cfg.seq_p // P, 192], BF16)
    kbt = sb("kbt", [P, 6, P], BF16)
    vbt = sb("vbt", [P, 6, 192], BF16)
    pt = [sb("pt%d" % i, [P, TILE], BF16) for i in range(4)]
    sq = [sb("sq%d" % i, [P, TILE], BF16) for i in range(3)]
    sg = [sb("sg%d" % i, [P, TILE], F32) for i in range(2)]
    rstd = sb("rstd", [P, TILE], F32)
    t1 = sb("t1", [P, TILE], F32)
    t2 = sb("t2", [P, TILE], F32)
    Rt, Ct = t1, t2
    etab = [sb("etab%d" % i, [P, TILE], BF16) for i in range(6)]
    sinkx = sb("sinkx", [P, TILE], F32)
    sx4 = sb("sx4", [P, 4], F32)
    cst = sb("cst", [P, cfg.ncst], F32)
    swp = sb("swp_sb", [P, P], F32)
    ones_bf = sb("ones_bf", [P, P], BF16)
    bones_bf = sb("bones_bf", [P, P], BF16)
    ksa = sb("ksa", [P, TILE], BF16)
    qst = [sb("qst%d" % i, [P, TILE], BF16) for i in range(2)]
    ksb = sb("ksb", [P, TILE], BF16)
    vstA = sb("vstA", [P, 4, 192], BF16)
    vstB = sb("vstB", [P, 4, 192], BF16)
    hlk = sb("hlk", [P, 4, P], BF16)
    hlv = sb("hlv", [P, 4, 192], BF16)
    hacc = sb("hacc", [P, 192], F32)
    ps = [es.enter_context(nc.psum_tensor("ps%d" % i, [P, TILE], F32)) for i in range(8)]

    def PS(b):
        return ("ps", b)

    T.dma("gq", lambda e: e.dma_start(out=cst[:], in_=cst_d[:, :]), writes=["cst"])
    T.dma("gq", lambda e: e.dma_start(out=swp[:], in_=swp_d[:, :]), writes=["swp"])
    T.op("pool", lambda e: e.memset(ones_bf[:], 1.0), writes=["ones"])
    T.op("pool", lambda e: e.memset(bones_bf[:], 0.0), writes=["bones"])
    T.op("pool", lambda e: e.memset(bones_bf[0:64, 0:64], 1.0), writes=["bones"])
    T.op("pool", lambda e: e.memset(bones_bf[64:128, 64:128], 1.0), writes=["bones"])
    T.op("pool", lambda e: e.memset(vstA[:], 1.0), writes=["vstA"])
    T.op("pool", lambda e: e.memset(vstB[:], 1.0), writes=["vstB"])
    order = ["wg1", "wu1", "wd1", "win", "wout", "wg2", "wu2", "wd2"]
    pending_casts = [(l, nme) for l in range(L) for nme in order]

    def issue_casts(n):
        for _ in range(n):
            if not pending_casts:
                return
            l, nme = pending_casts.pop(0)
            r, w = wnames[nme]
            nsplit = 2 if r > 2048 else 1
            rr = r // nsplit
            for s_ in range(nsplit):
                lo = l * r + s_ * rr
                T.dma("gq", (lambda e, nme=nme, lo=lo, rr=rr: e.dma_start(out=wbf[nme][lo:lo + rr, :], in_=wf32[nme][lo:lo + rr, :])),
                      writes=[("wbf", nme, l, s_)])

    T.dma("gq", lambda e: e.dma_start(out=x_sb[:], in_=xT.rearrange("(c p) t -> p c t", p=P)[:, :, 0:TILE]),
          writes=[("x", c) for c in range(KC)])
    issue_casts(4)
    for i in range(6):
        T.dma("gq", (lambda e, i=i: e.dma_start(out=t1[:], in_=ebias_d[i, :, :])), writes=["t1"])
        T.op("act", (lambda e, i=i: e.activation(out=etab[i][:], in_=t1[:], func=AF.Exp)), reads=["t1"], writes=[("etab", i)])

    wctr = [0]

    def wnext(nme, l, unit):
        r, w = wnames[nme]
        slot = wctr[0] % NSLOT
        wctr[0] += 1
        row = l * r + unit * P
        half = (unit * P) // (r // 2) if r > 2048 else 0
        T.dma("sync", (lambda e: e.dma_start(out=wsl[slot][:, 0:w], in_=wbf[nme][row:row + P, :])),
              reads=[("wbf", nme, l, half)], writes=[("ws", slot)], slot=slot)
        return slot

    def wview(slot, nk):
        return wsl[slot][:, 0:nk * P].rearrange("p (k m) -> p k m", m=P)

    def cc(l, off, n=1):
        return cst[:, l * LW + off:l * LW + off + n]

    sqi = [0]

    pend = []

    def flush_stats():
        while pend:
            pend.pop(0)()

    def stats_acc(src_ap, src_keys, first, last, lhs=None, lhs_key="ones", bank=7, defer=False):
        i = sqi[0] % 3
        sqi[0] += 1
        T.op("act", lambda e: e.activation(out=sq[i][:], in_=src_ap, func=AF.Square), reads=src_keys, writes=[("sq", i)])
        lh = ones_bf if lhs is None else lhs

        def emit():
            T.op("pe", lambda e: e.matmul(ps[bank][:], lhsT=lh[:], rhs=sq[i][:], start=first, stop=last),
                 reads=[("sq", i), lhs_key], writes=[PS(bank)])
        if defer:
            pend.append(emit)
        else:
            emit()

    def rstd_from(bank, scale, bias):
        T.op("act", lambda e: e.activation(out=rstd[:], in_=ps[bank][:], func=AF.Sqrt, scale=scale, bias=bias),
             reads=[PS(bank)], writes=["rstd"])
        T.op("dve", lambda e: e.reciprocal(out=rstd[:], in_=rstd[:]), reads=["rstd"], writes=["rstd"])

    def prenorm(l, goff):
        for c in range(KC):
            stats_acc(x_sb[:, c, :], [("x", c)], c == 0, c == KC - 1)
        rstd_from(7, 1.0 / D, EPS)
        for c in range(KC):
            T.op("dve", (lambda e, c=c: e.scalar_tensor_tensor(out=xn[:, c, :], in0=x_sb[:, c, :], scalar=cc(l, goff + c),
                                                                 in1=rstd[:], op0=ALU.mult, op1=ALU.mult)),
                 reads=[("x", c), "rstd", "cst"], writes=[("xn", c)])

    def postnorm_resid(l, goff, half):
        if half:
            rstd_from(7, 4.0 / D, 4.0 * EPS)
        else:
            rstd_from(7, 1.0 / D, EPS)
        for m in range(KC):
            T.op("dve", (lambda e, m=m: e.scalar_tensor_tensor(out=t1[:], in0=yo[:, m, :], scalar=cc(l, goff + m),
                                                                 in1=rstd[:], op0=ALU.mult, op1=ALU.mult)),
                 reads=[("yo", m), "rstd", "cst"], writes=["t1"])
            T.op("dve", (lambda e, m=m: e.tensor_tensor(out=x_sb[:, m, :], in0=x_sb[:, m, :], in1=t1[:], op=ALU.add)),
                 reads=[("x", m), "t1"], writes=[("x", m)])

    def evac_y(m, bank, first, last):
        T.op("act", lambda e: e.activation(out=yo[:, m, :], in_=ps[bank][:], func=AF.Copy), reads=[PS(bank)], writes=[("yo", m)])
        stats_acc(ps[bank][:], [PS(bank)], first, last, defer=True)

    def ffn(l, which):
        sfx = "1" if which == 1 else "2"
        gpre, gpost = (0, 8) if which == 1 else (32, 40)
        prenorm(l, gpre)
        for f in range(NF):
            bg, bu = f % 2, 2 + f % 2
            for nme, bank in (("wg" + sfx, bg), ("wu" + sfx, bu)):
                slot = wnext(nme, l, f)
                wv = wview(slot, KC)

                if f == 0:
                    for k in range(KC):
                        T.op("pe", (lambda e, wv=wv, bank=bank, k=k: e.matmul(ps[bank][:], lhsT=wv[:, k, :], rhs=xn[:, k, :],
                                                                                start=(k == 0), stop=(k == KC - 1))),
                             reads=[("ws", slot), ("xn", k)], writes=[PS(bank)])
                    continue

                def mm(e, wv=wv, bank=bank):
                    for k in range(KC):
                        ins = e.matmul(ps[bank][:], lhsT=wv[:, k, :], rhs=xn[:, k, :], start=(k == 0), stop=(k == KC - 1))
                    return ins
                T.op("pe", mm, reads=[("ws", slot)] + [("xn", c) for c in range(KC)], writes=[PS(bank)])
            T.op("act", (lambda e, f=f, bg=bg: e.activation(out=sg[f % 2][:], in_=ps[bg][:], func=AF.Silu)),
                 reads=[PS(bg)], writes=[("sg", f % 2)])
            T.op("dve", (lambda e, f=f, bu=bu: e.tensor_tensor(out=h_sb[:, f, :], in0=ps[bu][:], in1=sg[f % 2][:], op=ALU.mult)),
                 reads=[PS(bu), ("sg", f % 2)], writes=[("h", f)])
        for m in range(KC):
            bank = 4 + m % 2
            s0 = wnext("wd" + sfx, l, 2 * m)
            s1 = wnext("wd" + sfx, l, 2 * m + 1)
            v0, v1 = wview(s0, 11), wview(s1, 11)

            def mm(e, v0=v0, v1=v1, bank=bank):
                for f in range(NF):
                    wv = v0 if f < 11 else v1
                    ins = e.matmul(ps[bank][:], lhsT=wv[:, f % 11, :], rhs=h_sb[:, f, :], start=(f == 0), stop=(f == NF - 1))
                return ins
            T.op("pe", mm, reads=[("ws", s0), ("ws", s1)] + [("h", f) for f in range(NF)], writes=[PS(bank)])
            flush_stats()
            evac_y(m, bank, m == 0, m == KC - 1)
        flush_stats()
        postnorm_resid(l, gpost, True)

    def inproj(l, t, next_x_from_input=False):
        t0 = t * TILE
        part = 0 if t < cfg.ntp else 1
        lt = t if part == 0 else t - cfg.ntp
        nlt = cfg.ntp if part == 0 else cfg.nts
        prenorm(l, 16)
        T.dma("gq", lambda e: e.dma_start(out=xres.rearrange("(c p) t -> p c t", p=P)[:, :, t0:t0 + TILE], in_=x_sb[:]),
              reads=[("x", c) for c in range(KC)], writes=[("xres", t)])
        if next_x_from_input and t + 1 < cfg.ntile:
            t1_ = (t + 1) * TILE
            T.dma("gq", lambda e: e.dma_start(out=x_sb[:], in_=xT.rearrange("(c p) t -> p c t", p=P)[:, :, t1_:t1_ + TILE]),
                  writes=[("x", c) for c in range(KC)])
        T.dma("gq", lambda e: e.dma_start(out=sg[0][:], in_=cs_d[0, :, t0:t0 + TILE]), writes=[("sg", 0)])
        T.dma("gq", lambda e: e.dma_start(out=sg[1][:], in_=cs_d[1, :, t0:t0 + TILE]), writes=[("sg", 1)])

        first_proj = [True]

        def proj(unit, bank):
            slot = wnext("win", l, unit)
            wv = wview(slot, KC)
            if first_proj[0]:
                first_proj[0] = False
                for k in range(KC):
                    T.op("pe", (lambda e, k=k: e.matmul(ps[bank][:], lhsT=wv[:, k, :], rhs=xn[:, k, :], start=(k == 0), stop=(k == KC - 1))),
                         reads=[("ws", slot), ("xn", k)], writes=[PS(bank)])
                return

            def mm(e):
                for k in range(KC):
                    ins = e.matmul(ps[bank][:], lhsT=wv[:, k, :], rhs=xn[:, k, :], start=(k == 0), stop=(k == KC - 1))
                return ins
            T.op("pe", mm, reads=[("ws", slot)] + [("xn", c) for c in range(KC)], writes=[PS(bank)])

        def roped_a(u_main, u_sw, par, goff, out_ap, out_keys):
            b0, b1 = 2 * par, 2 * par + 1
            proj(u_main, b0)
            proj(u_sw, b1)
            stats_acc(ps[b0][:], [PS(b0)], True, True, lhs=bones_bf, lhs_key="bones", bank=6, defer=True)
            return pend.pop()

        def roped_b(stat_emit, u_main, u_sw, par, goff, out_ap, out_keys):
            b0, b1 = 2 * par, 2 * par + 1
            stat_emit()
            rstd_from(6, 1.0 / HD, EPS)
            T.op("dve", lambda e: e.scalar_tensor_tensor(out=t1[:], in0=ps[b0][:], scalar=cc(l, goff), in1=sg[0][:],
                                                         op0=ALU.mult, op1=ALU.mult), reads=[PS(b0), ("sg", 0), "cst"], writes=["t1"])
            T.op("dve", lambda e: e.scalar_tensor_tensor(out=t2[:], in0=ps[b1][:], scalar=cc(l, goff + 1), in1=sg[1][:],
                                                         op0=ALU.mult, op1=ALU.mult), reads=[PS(b1), ("sg", 1), "cst"], writes=["t2"])
            T.op("dve", lambda e: e.tensor_tensor(out=t1[:], in0=t1[:], in1=t2[:], op=ALU.add), reads=["t1", "t2"], writes=["t1"])
            T.op("dve", lambda e: e.tensor_tensor(out=out_ap, in0=t1[:], in1=rstd[:], op=ALU.mult), reads=["t1", "rstd"], writes=out_keys)

        qsc = q_scr[l].rearrange("(c p) t -> p c t", p=P)

        def qstore(c):
            i = c % 2
            T.dma("gq", lambda e: e.dma_start(out=qsc[:, c, t0:t0 + TILE], in_=qst[i][:]), reads=[("qst", i)], writes=[("q_scr", l, t, c)])

        rch = [(j, 4 + j, j % 3, 48, qst[j % 2][:], [("qst", j % 2)]) for j in range(4)] + [(8, 9, 4 % 3, 50, ksa[:], ["ksa"])]
        flush_stats()
        semit = {}
        for i in range(min(2, len(rch))):
            semit[i] = roped_a(*rch[i])
        for i in range(len(rch)):
            if i + 2 < len(rch):
                semit[i + 2] = roped_a(*rch[i + 2])
            roped_b(semit[i], *rch[i])
            if i < 4:
                qstore(i)
        for j in range(4):
            bank = j % 4
            proj(10 + j, bank)
            T.op("act", (lambda e, j=j, bank=bank: e.activation(out=qst[j % 2][:], in_=ps[bank][:], func=AF.Copy)),
                 reads=[PS(bank)], writes=[("qst", j % 2)])
            qstore(4 + j)
        proj(14, 0)
        T.op("act", lambda e: e.activation(out=ksb[:], in_=ps[0][:], func=AF.Copy), reads=[PS(0)], writes=["ksb"])
        sva = wnext("win", l, 15)
        svb = wnext("win", l, 16)
        wva, wvb = wview(sva, KC), wview(svb, KC)
        for blk in range(4):
            bank = 4 + blk % 2

            def mm(e, blk=blk, bank=bank):
                for wv, c0 in ((wva, 0), (wvb, P)):
                    for k in range(KC):
                        ins = e.matmul(ps[bank][:, c0:c0 + P], lhsT=xn[:, k, blk * P:(blk + 1) * P], rhs=wv[:, k, :],
                                       start=(k == 0), stop=(k == KC - 1))
                return ins
            T.op("pe", mm, reads=[("ws", sva), ("ws", svb)] + [("xn", c) for c in range(KC)], writes=[PS(bank)])
            for (dst, dkey, c0) in ((vstA, "vstA", 0), (vstB, "vstB", P)):
                for kv in range(2):
                    T.op("act", (lambda e, dst=dst, c0=c0, kv=kv, blk=blk, bank=bank: e.activation(
                        out=dst[:, blk, kv * 128:kv * 128 + 64], in_=ps[bank][:, c0 + kv * 64:c0 + kv * 64 + 64], func=AF.Copy)),
                        reads=[PS(bank)], writes=[dkey])
        gi, gs = t // 3, t % 3
        kin_ap = kin[l][gi].ap()
        q_written.add((l, t))
        T.dma("gq", lambda e: e.dma_start(out=kin_ap[:, gs * TSW:gs * TSW + TILE], in_=ksa[:]), reads=["ksa"], writes=[("kin", l, gi, gs, 0)])
        T.dma("gq", lambda e: e.dma_start(out=kin_ap[:, gs * TSW + TILE:(gs + 1) * TSW].rearrange("p (b c) -> p b c", c=192), in_=vstA[:]),
              reads=["vstA"], writes=[("kin", l, gi, gs, 1)])
        T.dma("gq", lambda e: e.dma_start(out=kvb_k[l][:, t0:t0 + TILE], in_=ksb[:]), reads=["ksb"], writes=[("kvbk", l, t)])
        T.dma("gq", lambda e: e.dma_start(out=kvb_v[l][:, t * 768:(t + 1) * 768].rearrange("p (b c) -> p b c", c=192), in_=vstB[:]),
              reads=["vstB"], writes=[("kvbv", l, t)])
        kb_ap = kinb[l][part].ap()
        if lt == 0:
            T.dma("gq", lambda e: e.dma_start(out=kb_ap[:, 0:128], in_=ksb[:, 0:128]), reads=["ksb"], writes=[("kinb", l, part, 0)])
            T.dma("gq", lambda e: e.dma_start(out=kb_ap[:, 256:448], in_=vstB[:, 0, :]), reads=["vstB"], writes=[("kinb", l, part, 2)])
        if lt == nlt - 1:
            T.dma("gq", lambda e: e.dma_start(out=kb_ap[:, 128:256], in_=ksb[:, 384:512]), reads=["ksb"], writes=[("kinb", l, part, 1)])
            T.dma("gq", lambda e: e.dma_start(out=kb_ap[:, 448:640], in_=vstB[:, 3, :]), reads=["vstB"], writes=[("kinb", l, part, 3)])
            deferred_ags.append(lambda: T.ag(lambda e: e.collective_compute("AllGather", ALU.bypass, replica_groups=GROUPS,
                                                                             ins=[kinb[l][part].ap().opt()], outs=[kallb[l][part].ap().opt()]),
                                             reads=[("kinb", l, part, i) for i in range(4)], writes=[("kallb", l, part)]))
        if gs == 2 or t == cfg.ntile - 1:
            def do_ag():
                ag_emitted.add((l, gi))
                T.ag(lambda e: e.collective_compute("AllGather", ALU.bypass, replica_groups=GROUPS,
                                                    ins=[kin[l][gi].ap().opt()], outs=[kall[l][gi].ap().opt()]),
                     reads=[("kin", l, gi, s_, i) for s_ in range(gs + 1) for i in range(2)], writes=[("kall", l, gi)])
            deferred_ags.append(do_ag)

    def fixup(a_num, a_rs_key, out_lo, out_hi, out_keys, sink):
        fixup1(a_num, a_rs_key, out_lo, out_hi, out_keys, sink)
        fixup2(a_num, a_rs_key, out_lo, out_hi, out_keys, sink)

    def fixup1(a_num, a_rs_key, out_lo, out_hi, out_keys, sink):
        if sink:
            T.op("dve", lambda e: e.tensor_tensor(out=Rt[64:128, :], in0=ps[4][64:128, :], in1=sinkx[64:128, :], op=ALU.add),
                 reads=[PS(4), "sinkx"], writes=["t1"])
            T.op("dve", lambda e: e.tensor_tensor(out=Rt[0:64, :], in0=ps[5][0:64, :], in1=sinkx[0:64, :], op=ALU.add),
                 reads=[PS(5), "sinkx"], writes=["t1"])
        else:
            T.op("dve", lambda e: e.tensor_copy(out=Rt[64:128, :], in_=ps[4][64:128, :]), reads=[PS(4)], writes=["t1"])
            T.op("dve", lambda e: e.tensor_copy(out=Rt[0:64, :], in_=ps[5][0:64, :]), reads=[PS(5)], writes=["t1"])
        T.op("act", lambda e: e.activation(out=out_lo, in_=a_num(ps[4][0:64, :]), func=AF.Copy), reads=[PS(4)], writes=out_keys)
        T.op("act", lambda e: e.activation(out=out_hi, in_=a_num(ps[5][64:128, :]), func=AF.Copy), reads=[PS(5)], writes=out_keys)
        T.op("dve", lambda e: e.reciprocal(out=Rt[:], in_=Rt[:]), reads=["t1"], writes=["t1"])

    def fixup2(a_num, a_rs_key, out_lo, out_hi, out_keys, sink):
        T.op("pe", lambda e: e.matmul(ps[6][:], lhsT=swp[:], rhs=Rt[:], start=True, stop=True), reads=["t1", "swp"], writes=[PS(6)])
        T.op("act", lambda e: e.activation(out=Ct[:], in_=ps[6][:], func=AF.Copy), reads=[PS(6)], writes=["t2"])
        T.op("dve", lambda e: e.tensor_tensor(out=out_lo, in0=out_lo, in1=a_num(Ct[0:64, :]), op=ALU.mult),
             reads=list(out_keys) + ["t2"], writes=out_keys)
        T.op("dve", lambda e: e.tensor_tensor(out=out_hi, in0=out_hi, in1=a_num(Ct[64:128, :]), op=ALU.mult),
             reads=list(out_keys) + ["t2"], writes=out_keys)

    kv_loaded = set()
    ag_emitted = set()
    deferred_ags = []

    def flush_ags():
        while deferred_ags:
            deferred_ags.pop(0)()

    def load_kv_resident(l, part):
        tiles = range(cfg.ntp) if part == 0 else range(cfg.ntp, cfg.ntile)
        if (l, part) in kv_loaded or not all((l, tt // 3) in ag_emitted for tt in tiles):
            return
        kv_loaded.add((l, part))
        tpart = cfg.TP if part == 0 else cfg.TS
        kview = KaT[:, 0:4 * tpart].rearrange("p (r n) -> p r n", r=4)
        vview = VaA[:, 0:4 * tpart // P, :].rearrange("p (r n) c -> p r (n c)", r=4)
        for tt in tiles:
            ltt = tt if part == 0 else tt - cfg.ntp
            gi, gs = tt // 3, tt % 3
            src = kall[l][gi].ap().rearrange("(r p) w -> p r w", p=P)
            T.dma("gq", (lambda e, src=src, gs=gs, ltt=ltt: e.dma_start(out=kview[:, :, ltt * TILE:(ltt + 1) * TILE],
                                                                        in_=src[:, :, gs * TSW:gs * TSW + TILE])),
                  reads=[("kall", l, gi)], writes=["KaT"])
            T.dma("gq", (lambda e, src=src, gs=gs, ltt=ltt: e.dma_start(out=vview[:, :, ltt * 768:(ltt + 1) * 768],
                                                                        in_=src[:, :, gs * TSW + TILE:(gs + 1) * TSW])),
                  reads=[("kall", l, gi)], writes=["VaA"])

    def halo(l, part, left):
        src = kallb[l][part].ap().rearrange("(r p) w -> p r w", p=P)
        kc0, vc0 = (128, 448) if left else (0, 256)
        slot = 0 if left else 5
        so = L * LW + (0 if left else 4)
        T.dma("gq", lambda e: e.dma_start(out=hlk[:], in_=src[:, :, kc0:kc0 + 128]), reads=[("kallb", l, part)], writes=["hlk"])
        T.dma("gq", lambda e: e.dma_start(out=hlv[:], in_=src[:, :, vc0:vc0 + 192]), reads=[("kallb", l, part)], writes=["hlv"])
        for (srct, skey, dst, dkey, w) in ((hlk, "hlk", kbt, "kbt", 128), (hlv, "hlv", vbt, "vbt", 192)):
            T.op("dve", (lambda e, srct=srct, w=w: e.tensor_scalar(out=hacc[:, 0:w], in0=srct[:, 0, :], scalar1=cst[:, so:so + 1],
                                                                    scalar2=None, op0=ALU.mult)),
                 reads=[skey, "cst"], writes=["hacc"])
            for r_ in range(1, 4):
                o_ap = hacc[:, 0:w] if r_ < 3 else dst[:, slot, :]
                T.op("dve", (lambda e, srct=srct, w=w, r_=r_, o_ap=o_ap: e.scalar_tensor_tensor(
                    out=o_ap, in0=srct[:, r_, :], scalar=cst[:, so + r_:so + r_ + 1], in1=hacc[:, 0:w], op0=ALU.mult, op1=ALU.add)),
                    reads=[skey, "cst", "hacc"], writes=["hacc"] if r_ < 3 else [dkey])

    q_written = set()
    q_loaded = [None]

    def load_q(l, t):
        if q_loaded[0] == (l, t) or (l, t) not in q_written:
            return
        q_loaded[0] = (l, t)
        tq = t * TILE
        T.dma("gq", lambda e: e.dma_start(out=qbuf[:], in_=q_scr[l].rearrange("(c p) t -> p c t", p=P)[:, :, tq:tq + TILE]),
              reads=[("q_scr", l, t, c) for c in range(KC)], writes=[("q", c) for c in range(KC)])

    def attn(l, t):
        t0 = t * TILE
        part = 0 if t < cfg.ntp else 1
        lt = t if part == 0 else t - cfg.ntp
        nlt = cfg.ntp if part == 0 else cfg.nts
        seq = cfg.seq_p if part == 0 else cfg.seq_s
        NK = seq // P
        if lt == 0:
            flush_ags()
            load_kv_resident(l, part)
            assert (l, part) in kv_loaded
        load_q(l, t)
        assert q_loaded[0] == (l, t)
        T.dma("gq", lambda e: e.dma_start(out=x_sb[:], in_=xres.rearrange("(c p) t -> p c t", p=P)[:, :, t0:t0 + TILE]),
              reads=[("xres", t)], writes=[("x", c) for c in range(KC)])
        T.dma("gq", lambda e: e.dma_start(out=kbt[:, 1:5, :], in_=kvb_k[l][:, t0:t0 + TILE].rearrange("p (b c) -> p b c", c=P)),
              reads=[("kvbk", l, t)], writes=["kbt"])
        T.dma("gq", lambda e: e.dma_start(out=vbt[:, 1:5, :], in_=kvb_v[l][:, t * 768:(t + 1) * 768].rearrange("p (b c) -> p b c", c=192)),
              reads=[("kvbv", l, t)], writes=["vbt"])
        if lt > 0:
            T.dma("gq", lambda e: e.dma_start(out=kbt[:, 0, :], in_=kvb_k[l][:, t0 - P:t0]), reads=[("kvbk", l, t - 1)], writes=["kbt"])
            T.dma("gq", lambda e: e.dma_start(out=vbt[:, 0, :], in_=kvb_v[l][:, t * 768 - 192:t * 768]), reads=[("kvbv", l, t - 1)], writes=["vbt"])
        else:
            halo(l, part, True)
        if lt < nlt - 1:
            T.dma("gq", lambda e: e.dma_start(out=kbt[:, 5, :], in_=kvb_k[l][:, t0 + TILE:t0 + TILE + P]), reads=[("kvbk", l, t + 1)], writes=["kbt"])
            T.dma("gq", lambda e: e.dma_start(out=vbt[:, 5, :], in_=kvb_v[l][:, (t + 1) * 768:(t + 1) * 768 + 192]), reads=[("kvbv", l, t + 1)], writes=["vbt"])
        else:
            halo(l, part, False)

        if l == 0 and t == 0:
            issue_casts(8)
        if l == 0 and (t == 1 or cfg.ntile == 1):
            issue_casts(len(pending_casts))
        flush_ags()
        v4 = lambda ap: ap.rearrange("p (h q) -> p h q", h=4)

        def b_steps(qb):
            steps = [(kv, oi) for kv in range(2) for oi in range(3)]
            LA = 2
            for si in range(len(steps) + LA):
                if si < len(steps):
                    kv, oi = steps[si]
                    lo = kv * 64
                    eidx = kv * 3 + oi
                    ks = qb + oi
                    sbank = si % 4
                    T.op("pe", (lambda e, sbank=sbank, lo=lo, ks=ks: e.matmul(
                        ps[sbank][:], lhsT=kbt[lo:lo + 64, ks, :], rhs=qbuf[lo:lo + 64, 4:8, qb * P:(qb + 1) * P], start=True, stop=True)),
                        reads=["kbt"] + [("q", 4 + jj) for jj in range(4)], writes=[PS(sbank)])
                    T.op("act", (lambda e, sbank=sbank: e.activation(out=sg[sbank % 2][:], in_=ps[sbank][:], func=AF.Exp, scale=0.125)),
                         reads=[PS(sbank)], writes=[("sg", sbank % 2)])
                    T.op("dve", (lambda e, sbank=sbank, eidx=eidx: e.tensor_tensor(out=pt[sbank][:], in0=sg[sbank % 2][:], in1=etab[eidx][:], op=ALU.mult)),
                         reads=[("sg", sbank % 2), ("etab", eidx)], writes=[("pt", sbank)])
                if si >= LA:
                    kv, oi = steps[si - LA]
                    ks = qb + oi
                    sbank = (si - LA) % 4
                    T.op("pe", (lambda e, sbank=sbank, kv=kv, ks=ks, oi=oi: e.matmul(
                        ps[4 + kv][:], lhsT=vbt[:, ks, kv * 64:kv * 64 + 128], rhs=pt[sbank][:], start=(oi == 0), stop=(oi == 2))),
                        reads=["vbt", ("pt", sbank)], writes=[PS(4 + kv)])

        def b_args(qb):
            return (v4, None, yo[0:64, 4:8, qb * P:(qb + 1) * P], yo[64:128, 4:8, qb * P:(qb + 1) * P],
                    [("yo", 4 + jj) for jj in range(4)], True)

        ident = lambda ap: ap
        pend_b2 = []
        for j in range(4):
            for kc in range(NK + 1):
                if kc == min(8, NK) and pend_b2:
                    fixup2(*pend_b2.pop())
                if kc < NK:
                    for hh in range(2):
                        b = (kc % 2) * 2 + hh
                        lo = hh * 64
                        T.op("pe", (lambda e, b=b, lo=lo, kc=kc, j=j: e.matmul(ps[b][:], lhsT=KaT[lo:lo + 64, kc * P:(kc + 1) * P],
                                                                                 rhs=qbuf[lo:lo + 64, j, :], start=True, stop=True)),
                             reads=["KaT", ("q", j)], writes=[PS(b)])
                    for hh in range(2):
                        b = (kc % 2) * 2 + hh
                        T.op("act", (lambda e, b=b: e.activation(out=pt[b][:], in_=ps[b][:], func=AF.Exp, scale=0.125)),
                             reads=[PS(b)], writes=[("pt", b)])
                if kc >= 1:
                    k1 = kc - 1
                    for hh in range(2):
                        b = (k1 % 2) * 2 + hh
                        T.op("pe", (lambda e, b=b, hh=hh, k1=k1: e.matmul(ps[4 + hh][:], lhsT=VaA[:, k1, hh * 64:hh * 64 + 128], rhs=pt[b][:],
                                                                            start=(k1 == 0), stop=(k1 == NK - 1))),
                             reads=["VaA", ("pt", b)], writes=[PS(4 + hh)])
            a_args = (ident, None, yo[0:64, j, :], yo[64:128, j, :], [("yo", j)], False)
            fixup1(*a_args)
            b_steps(j)
            fixup2(*a_args)
            stats_acc(yo[:, j, :], [("yo", j)], j == 0, j == 3)
            if j == 3:
                rstd_from(7, 1.0 / 512, EPS)
                for jj in range(4):
                    T.op("dve", (lambda e, jj=jj: e.scalar_tensor_tensor(out=xn[:, jj, :], in0=yo[:, jj, :], scalar=cc(l, 52 + jj), in1=rstd[:],
                                                                           op0=ALU.mult, op1=ALU.mult)),
                         reads=[("yo", jj), "rstd", "cst"], writes=[("xn", jj)])
            fixup1(*b_args(j))
            pend_b2.append(b_args(j))
        while pend_b2:
            fixup2(*pend_b2.pop())
        if lt == nlt - 1:
            if part == 0:
                load_kv_resident(l, 1)
            elif l + 1 < L:
                load_kv_resident(l + 1, 0)

        if t + 1 < cfg.ntile:
            load_q(l, t + 1)
        elif l + 1 < L:
            load_q(l + 1, 0)
        for j in range(4):
            stats_acc(yo[:, 4 + j, :], [("yo", 4 + j)], j == 0, j == 3)
        rstd_from(7, 1.0 / 512, EPS)
        for j in range(4):
            T.op("dve", (lambda e, j=j: e.scalar_tensor_tensor(out=xn[:, 4 + j, :], in0=yo[:, 4 + j, :], scalar=cc(l, 56 + j), in1=rstd[:],
                                                                 op0=ALU.mult, op1=ALU.mult)),
                 reads=[("yo", 4 + j), "rstd", "cst"], writes=[("xn", 4 + j)])

        for m in range(KC):
            bank = 4 + m % 2
            slot = wnext("wout", l, m)
            wv = wview(slot, KC)

            def mm(e, wv=wv, bank=bank):
                for k in range(KC):
                    ins = e.matmul(ps[bank][:], lhsT=wv[:, k, :], rhs=xn[:, k, :], start=(k == 0), stop=(k == KC - 1))
                return ins
            if m == 0:
                for k in range(KC):
                    T.op("pe", (lambda e, wv=wv, bank=bank, k=k: e.matmul(ps[bank][:], lhsT=wv[:, k, :], rhs=xn[:, k, :],
                                                                            start=(k == 0), stop=(k == KC - 1))),
                         reads=[("ws", slot), ("xn", k)], writes=[PS(bank)])
            else:
                T.op("pe", mm, reads=[("ws", slot)] + [("xn", c) for c in range(KC)], writes=[PS(bank)])
            flush_stats()
            evac_y(m, bank, m == 0, m == KC - 1)
        flush_stats()
        postnorm_resid(l, 24, False)

    def layer_consts(l):
        T.op("act", lambda e: e.activation(out=sx4[:], in_=cc(l, 60, 4), func=AF.Exp), reads=["cst"], writes=["sx4"])
        T.op("pool", lambda e: e.memset(sinkx[:], 0.0), writes=["sinkx"])
        for j in range(4):
            T.op("dve", (lambda e, j=j: e.tensor_scalar(out=sinkx[:, j * P:(j + 1) * P], in0=sinkx[:, j * P:(j + 1) * P],
                                                         scalar1=sx4[:, j:j + 1], scalar2=None, op0=ALU.add)),
                 reads=["sx4", "sinkx"], writes=["sinkx"])

    ncast_per_tile = (len(pending_casts) + cfg.ntile - 1) // cfg.ntile
    for t in range(cfg.ntile):
        t0 = t * TILE
        flush_ags()
        ffn(0, 1)
        inproj(0, t, next_x_from_input=True)
    for l in range(L):
        layer_consts(l)
        for t in range(cfg.ntile):
            t0 = t * TILE
            attn(l, t)
            ffn(l, 2)
            if l + 1 < L:
                ffn(l + 1, 1)
                inproj(l + 1, t)
            else:
                T.dma("gq", (lambda e, t0=t0: e.dma_start(out=yT.rearrange("(c p) t -> p c t", p=P)[:, :, t0:t0 + TILE], in_=x_sb[:])),
                      reads=[("x", c) for c in range(KC)], writes=[("yT", t)])
    T.finish()

    sems = {n: es.enter_context(nc.semaphore(n)) for n in T.sem_names()}
    with nc.Block() as block:
        @block.tensor
        def _(e):
            replay(T.q["pe"], e, sems)

        @block.scalar
        def _(e):
            replay(T.q["act"], e, sems)

        @block.vector
        def _(e):
            replay(T.q["dve"], e, sems)

        @block.gpsimd
        def _(e):
            replay(T.q["pool"], e, sems)

        @block.sync
        def _(e):
            replay(T.q["sync"], e, sems)
    es.close()
    return nc, T


def _units_k1024(w, cols_list):
    out = []
    for cols in cols_list:
        u = w[:, cols].reshape(KC, P, P).transpose(1, 0, 2).reshape(P, KC * P)
        out.append(u)
    return np.concatenate(out, axis=0)


def _prep_weights(inp, L):
    f32 = np.float32
    permd = np.concatenate([np.arange(0, 64, 2), np.arange(1, 64, 2)])
    swd = permd[(np.arange(64) + 32) % 64]
    ar = np.arange(64)
    res = {k: [] for k in ("wg1", "wu1", "wd1", "win", "wout", "wg2", "wu2", "wd2")}
    for l in range(L):
        ffnw = {"1": (inp["ffn1_w_gate"], inp["ffn1_w_up"], inp["ffn1_w_down"]),
                "2": (inp["ffn2_w_gate"], inp["ffn2_w_up"], inp["ffn2_w_down"])}
        for sfx in ("1", "2"):
            wg = np.asarray(ffnw[sfx][0][l], f32)
            wu = np.asarray(ffnw[sfx][1][l], f32)
            wd = np.asarray(ffnw[sfx][2][l], f32)
            cl = [np.arange(f * P, (f + 1) * P) for f in range(NF)]
            res["wg" + sfx].append(_units_k1024(wg, cl))
            res["wu" + sfx].append(_units_k1024(wu, cl))
            units = []
            for m in range(KC):
                for hf in range(2):
                    blk = wd[hf * 11 * P:(hf + 1) * 11 * P, m * P:(m + 1) * P]
                    units.append(blk.reshape(11, P, P).transpose(1, 0, 2).reshape(P, 11 * P))
            res["wd" + sfx].append(np.concatenate(units, axis=0))
        w_in = np.asarray(inp["w_in"][l], f32)
        qa, ka, va, qb, kb, vb = 0, 512, 640, 768, 1280, 1408
        cl = []
        for j in range(4):
            cl.append(np.concatenate([qa + j * 64 + permd, qa + (4 + j) * 64 + permd]))
        for j in range(4):
            cl.append(np.concatenate([qa + j * 64 + swd, qa + (4 + j) * 64 + swd]))
        cl.append(np.concatenate([ka + permd, ka + 64 + permd]))
        cl.append(np.concatenate([ka + swd, ka + 64 + swd]))
        for j in range(4):
            cl.append(np.concatenate([qb + j * 64 + ar, qb + (4 + j) * 64 + ar]))
        cl.append(kb + np.arange(P))
        cl.append(va + np.arange(P))
        cl.append(vb + np.arange(P))
        res["win"].append(_units_k1024(w_in, cl))
        w_out = np.asarray(inp["w_out"][l], f32)
        rows = []
        for c in range(8):
            base = 0 if c < 4 else 512
            j = c % 4
            rows.append(np.concatenate([base + j * 64 + ar, base + (4 + j) * 64 + ar]))
        rows = np.concatenate(rows)
        wo = w_out[rows, :]
        units = []
        for m in range(KC):
            units.append(wo[:, m * P:(m + 1) * P].reshape(KC, P, P).transpose(1, 0, 2).reshape(P, KC * P))
        res["wout"].append(np.concatenate(units, axis=0))
    return {k: np.ascontiguousarray(np.concatenate(v, axis=0)) for k, v in res.items()}, permd, swd


def _prep_consts(inp, cfg, permd, swd):
    L = cfg.depth
    f32 = np.float32
    base = np.zeros((P, cfg.ncst), f32)
    pidx = np.arange(P)

    def fm(v):
        return np.asarray(v, f32).reshape(KC, P).T

    for l in range(L):
        o = l * LW
        base[:, o + 0:o + 8] = fm(inp["ffn1_pre"][l])
        base[:, o + 8:o + 16] = fm(inp["ffn1_post"][l])
        base[:, o + 16:o + 24] = fm(inp["mix_pre"][l])
        base[:, o + 24:o + 32] = fm(inp["mix_post"][l])
        base[:, o + 32:o + 40] = fm(inp["ffn2_pre"][l])
        base[:, o + 40:o + 48] = fm(inp["ffn2_post"][l])
        gq = np.asarray(inp["a_q_norm"][l], f32)
        gk = np.asarray(inp["a_k_norm"][l], f32)
        base[:, o + 48] = gq[permd[pidx % 64]]
        base[:, o + 49] = gq[swd[pidx % 64]]
        base[:, o + 50] = gk[permd[pidx % 64]]
        base[:, o + 51] = gk[swd[pidx % 64]]
        ga = np.asarray(inp["a_out_norm"][l], f32)
        gb = np.asarray(inp["b_out_norm"][l], f32)
        sk = np.asarray(inp["b_sink"][l], f32)
        for j in range(4):
            hsel = np.where(pidx < 64, j, 4 + j)
            base[:, o + 52 + j] = ga[hsel * 64 + pidx % 64]
            base[:, o + 56 + j] = gb[hsel * 64 + pidx % 64]
            base[:, o + 60 + j] = np.where(pidx >= 64, sk[j], sk[4 + j])
    return base


def _rope_tables(cfg, r):
    f32 = np.float32
    inv_freq = (np.float32(10000.0) ** (-np.arange(16, dtype=f32) / np.float32(16))).astype(f32)
    pos = np.concatenate([r * cfg.TP + np.arange(cfg.TP), r * cfg.TS + np.arange(cfg.TS)])
    row = (pos // 64).astype(f32)
    col = (pos % 64).astype(f32)
    ang = np.concatenate([row[:, None] * inv_freq[None, :], col[:, None] * inv_freq[None, :]], axis=1).astype(f32)
    c, s = np.cos(ang).astype(f32), np.sin(ang).astype(f32)
    pidx = np.arange(P)
    pair = (pidx % 64) % 32
    sign = np.where((pidx % 64) < 32, -1.0, 1.0).astype(f32)
    cs = np.stack([c[:, pair].T, s[:, pair].T * sign[:, None]], axis=0)
    return np.ascontiguousarray(cs.astype(f32))


def _ebias(r):
    f32 = np.float32
    out = np.zeros((6, P, TILE), f32)
    k = np.arange(P)[:, None]
    q = np.arange(P)[None, :]
    for kv in range(2):
        for var in range(3):
            off = var - 1
            dist = np.abs(q - k - 128 * off).astype(f32)
            for j in range(4):
                h = kv * 4 + j
                slope = np.float32(2.0 ** (-(h + 1)))
                b = np.where(dist <= 128, -slope * dist, -30000.0).astype(f32)
                out[kv * 3 + var, :, j * P:(j + 1) * P] = b
    return out


_CACHE = {}


def kernel(**inp):
    xp = np.asarray(inp["x_prompt"], np.float32)
    xs = np.asarray(inp["x_sample"], np.float32)
    L = int(np.asarray(inp["w_in"]).shape[0])
    cfg = Cfg(xp.shape[1], xs.shape[1], L)
    key = (cfg.seq_p, cfg.seq_s, L)
    if key not in _CACHE:
        _CACHE[key] = build(cfg)[0]
    nc = _CACHE[key]
    wts, permd, swd = _prep_weights(inp, L)
    cbase = _prep_consts(inp, cfg, permd, swd)
    swpm = np.zeros((P, P), np.float32)
    swpm[np.arange(P), (np.arange(P) + 64) % P] = 1.0
    in_maps = []
    for c in range(8):
        g, r = c // 4, c % 4
        xt = np.concatenate([xp[g, r * cfg.TP:(r + 1) * cfg.TP, :], xs[g, r * cfg.TS:(r + 1) * cfg.TS, :]], axis=0)
        cstc = cbase.copy()
        so = L * LW
        if r > 0:
            cstc[:, so + r - 1] = 1.0
        if r < 3:
            cstc[:, so + 4 + r + 1] = 1.0
        m = {"xT": np.ascontiguousarray(xt.T), "cs": _rope_tables(cfg, r), "cst": cstc, "ebias": _ebias(r), "swp": swpm}
        m.update(wts)
        in_maps.append(m)
    res = run_bass_kernel_spmd(nc, in_maps, core_ids=list(range(8)))
    yp = np.zeros_like(xp)
    ys = np.zeros_like(xs)
    for c in range(8):
        g, r = c // 4, c % 4
        y = np.asarray(res.results[c]["yT"], np.float32).T
        yp[g, r * cfg.TP:(r + 1) * cfg.TP, :] = y[:cfg.TP]
        ys[g, r * cfg.TS:(r + 1) * cfg.TS, :] = y[cfg.TP:]
    return (yp, ys)
```

```python
import contextlib
import numpy as np
import ml_dtypes
import concourse.bass as bass
import concourse.mybir as mybir
from concourse.bass_utils import run_bass_kernel_spmd

F32 = mybir.dt.float32
BF16 = mybir.dt.bfloat16
AF = mybir.ActivationFunctionType
ALU = mybir.AluOpType

P = 128
TILE = 512
D = 1024
DFF = 2816
NF = DFF // P
KC = D // P
HD = 64
EPS = 1e-6
NSLOT = 5
SLOTW = 1408
GROUPS = [[0, 1, 2, 3], [4, 5, 6, 7]]
LW = 64
TSW = 1280


class Cfg:
    def __init__(self, seq_p, seq_s, depth):
        self.seq_p, self.seq_s, self.depth = seq_p, seq_s, depth
        self.TP, self.TS = seq_p // 4, seq_s // 4
        self.NT = self.TP + self.TS
        self.ntp, self.nts = self.TP // TILE, self.TS // TILE
        self.ntile = self.ntp + self.nts
        self.nag = (self.ntile + 2) // 3
        self.ncst = depth * LW + 8


COMPUTE = ("pe", "act", "dve", "pool")
QUEUES = {"sync": "sync", "gq": "pool"}


class Prog:
    def __init__(self):
        self.q = {e: [] for e in ("pe", "act", "dve", "pool", "sync")}
        self.cnt = {e: 0 for e in COMPUTE}
        self.waited = {e: {} for e in self.q}
        self.last_w = {}
        self.readers = {}
        self.dsem = {"sync": [0] * NSLOT, "gq": [0] * 12}
        self.drr = {"sync": 0, "gq": 0}
        self.nag = 0
        self.ninstr = 0

    def _wait(self, eng, ev):
        sem, val = ev[0], ev[1]
        if self.waited[eng].get(sem, 0) >= val:
            return
        self.waited[eng][sem] = val
        self.q[eng].append(("wait", sem, val))

    def _deps(self, eng, reads, writes, is_dma):
        evs = []
        for k in reads:
            e = self.last_w.get(k)
            if e is not None:
                evs.append((e, "raw"))
        for k in writes:
            e = self.last_w.get(k)
            if e is not None:
                evs.append((e, "waw"))
            for src, e in self.readers.get(k, {}).items():
                evs.append((e, "war"))
        for (sem, val, src), kind in evs:
            if not is_dma and src == eng:
                if eng == "pe" or kind != "raw":
                    continue
            self._wait(eng, (sem, val))

    def _commit(self, ev, reads, writes):
        for k in reads:
            d = self.readers.setdefault(k, {})
            key = ev[2] if ev[2] is not None else ev[0]
            d[key] = ev
        for k in writes:
            self.last_w[k] = ev
            self.readers[k] = {}

    def op(self, eng, fn, reads=(), writes=()):
        self._deps(eng, reads, writes, False)
        self.cnt[eng] += 1
        ev = ("s_" + eng, self.cnt[eng], eng)
        self.q[eng].append(("op", fn, "s_" + eng, 1))
        self._commit(ev, reads, writes)
        self.ninstr += 1

    def dma(self, queue, fn, reads=(), writes=(), slot=None):
        eng = QUEUES[queue]
        n = len(self.dsem[queue])
        if slot is None:
            slot = self.drr[queue]
            self.drr[queue] = (slot + 1) % n
        sem = "d_%s_%d" % (queue, slot)
        uses = self.dsem[queue][slot]
        if uses > 0:
            self._wait(eng, (sem, 16 * uses))
        self._deps(eng, reads, writes, True)
        self.dsem[queue][slot] = uses + 1
        ev = (sem, 16 * (uses + 1), None)
        self.q[eng].append(("op", fn, sem, 16))
        self._commit(ev, reads, writes)
        self.ninstr += 1

    def ag(self, fn, reads, writes):
        eng = "pool"
        self._deps(eng, reads, writes, True)
        sem = "ag_%d" % self.nag
        self.nag += 1
        ev = (sem, 1, None)
        self.q[eng].append(("ag", fn, sem, 1))
        self._commit(ev, reads, writes)

    def sem_names(self):
        names = ["s_" + e for e in COMPUTE]
        for qn, lst in self.dsem.items():
            names += ["d_%s_%d" % (qn, i) for i in range(len(lst))]
        names += ["ag_%d" % i for i in range(self.nag)]
        return names

    def finish(self):
        for qn, lst in self.dsem.items():
            for i, uses in enumerate(lst):
                if uses:
                    self._wait("pool", ("d_%s_%d" % (qn, i), 16 * uses))


def replay(items, eng, sems):
    for it in items:
        if it[0] == "wait":
            eng.wait_ge(sems[it[1]], it[2])
        elif it[0] == "op":
            ins = it[1](eng)
            ins.then_inc(sems[it[2]], it[3])
        else:
            ins = it[1](eng)
            ins.then_inc(sems[it[2]])


def build(cfg):
    nc = bass.Bass("TRN2", target_bir_lowering=False)
    T = Prog()
    L = cfg.depth
    NT = cfg.NT
    es = contextlib.ExitStack()

    def din(name, shape, dt=F32):
        return nc.dram_tensor(name, list(shape), dt, kind="ExternalInput").ap()

    def dscr(name, shape, dt):
        return nc.dram_tensor(name, list(shape), dt).ap()

    xT = din("xT", [D, NT])
    yT = nc.dram_tensor("yT", [D, NT], F32, kind="ExternalOutput").ap()
    cs_d = din("cs", [2, P, NT])
    cst_d = din("cst", [P, cfg.ncst])
    ebias_d = din("ebias", [6, P, TILE])
    swp_d = din("swp", [P, P])
    wnames = {"wg1": (NF * P, D), "wu1": (NF * P, D), "wd1": (16 * P, SLOTW), "win": (17 * P, D),
              "wout": (8 * P, D), "wg2": (NF * P, D), "wu2": (NF * P, D), "wd2": (16 * P, SLOTW)}
    wf32, wbf = {}, {}
    for nme, (r, w) in wnames.items():
        wf32[nme] = din(nme, [L * r, w])
        wbf[nme] = dscr(nme + "_bf", [L * r, w], BF16)

    xres = dscr("xres", [D, NT], F32)
    q_scr = [dscr("q_scr%d" % l, [KC * P, NT], BF16) for l in range(L)]
    kvb_k = [dscr("kvb_k%d" % l, [P, NT], BF16) for l in range(L)]
    kvb_v = [dscr("kvb_v%d" % l, [P, (NT // P) * 192], BF16) for l in range(L)]
    kin = [[nc.dram_tensor("kin%d_%d" % (l, i), [P, 3 * TSW], BF16) for i in range(cfg.nag)] for l in range(L)]
    kall = [[nc.dram_tensor("kall%d_%d" % (l, i), [4 * P, 3 * TSW], BF16) for i in range(cfg.nag)] for l in range(L)]
    kinb = [[nc.dram_tensor("kinb%d_%d" % (l, i), [P, 640], BF16) for i in range(2)] for l in range(L)]
    kallb = [[nc.dram_tensor("kallb%d_%d" % (l, i), [4 * P, 640], BF16) for i in range(2)] for l in range(L)]

    def sb(name, shape, dt):
        return es.enter_context(nc.sbuf_tensor("sb_" + name, list(shape), dt))

    x_sb = sb("x_sb", [P, KC, TILE], F32)
    yo = sb("yo", [P, KC, TILE], F32)
    xn = sb("xn", [P, KC, TILE], BF16)
    h_sb = sb("h_sb", [P, NF, TILE], BF16)
    qbuf = sb("qbuf", [P, KC, TILE], BF16)
    wsl = [sb("wsl%d" % i, [P, SLOTW], BF16) for i in range(NSLOT)]
    KaT = sb("KaT", [P, cfg.seq_p], BF16)
    VaA = sb("VaA", [P, cfg.seq_p // P, 192], BF16)
    kbt = sb("kbt", [P, 6, P], BF16)
    vbt = sb("vbt", [P, 6, 192], BF16)
    pt = [sb("pt%d" % i, [P, TILE], BF16) for i in range(4)]
    sq = [sb("sq%d" % i, [P, TILE], BF16) for i in range(3)]
    sg = [sb("sg%d" % i, [P, TILE], F32) for i in range(2)]
    rstd = sb("rstd", [P, TILE], F32)
    t1 = sb("t1", [P, TILE], F32)
    t2 = sb("t2", [P, TILE], F32)
    Rt, Ct = t1, t2
    etab = [sb("etab%d" % i, [P, TILE], BF16) for i in range(6)]
    sinkx = sb("sinkx", [P, TILE], F32)
    sx4 = sb("sx4", [P, 4], F32)
    cst = sb("cst", [P, cfg.ncst], F32)
    swp = sb("swp_sb", [P, P], F32)
    ones_bf = sb("ones_bf", [P, P], BF16)
    bones_bf = sb("bones_bf", [P, P], BF16)
    ksa = sb("ksa", [P, TILE], BF16)
    qst = [sb("qst%d" % i, [P, TILE], BF16) for i in range(2)]
    ksb = sb("ksb", [P, TILE], BF16)
    vstA = sb("vstA", [P, 4, 192], BF16)
    vstB = sb("vstB", [P, 4, 192], BF16)
    hlk = sb("hlk", [P, 4, P], BF16)
    hlv = sb("hlv", [P, 4, 192], BF16)
    hacc = sb("hacc", [P, 192], F32)
    ps = [es.enter_context(nc.psum_tensor("ps%d" % i, [P, TILE], F32)) for i in range(8)]

    def PS(b):
        return ("ps", b)

    T.dma("gq", lambda e: e.dma_start(out=cst[:], in_=cst_d[:, :]), writes=["cst"])
    T.dma("gq", lambda e: e.dma_start(out=swp[:], in_=swp_d[:, :]), writes=["swp"])
    T.op("pool", lambda e: e.memset(ones_bf[:], 1.0), writes=["ones"])
    T.op("pool", lambda e: e.memset(bones_bf[:], 0.0), writes=["bones"])
    T.op("pool", lambda e: e.memset(bones_bf[0:64, 0:64], 1.0), writes=["bones"])
    T.op("pool", lambda e: e.memset(bones_bf[64:128, 64:128], 1.0), writes=["bones"])
    T.op("pool", lambda e: e.memset(vstA[:], 1.0), writes=["vstA"])
    T.op("pool", lambda e: e.memset(vstB[:], 1.0), writes=["vstB"])
    order = ["wg1", "wu1", "wd1", "win", "wout", "wg2", "wu2", "wd2"]
    pending_casts = [(l, nme) for l in range(L) for nme in order]

    def issue_casts(n):
        for _ in range(n):
            if not pending_casts:
                return
            l, nme = pending_casts.pop(0)
            r, w = wnames[nme]
            nsplit = 2 if r > 2048 else 1
            rr = r // nsplit
            for s_ in range(nsplit):
                lo = l * r + s_ * rr
                T.dma("gq", (lambda e, nme=nme, lo=lo, rr=rr: e.dma_start(out=wbf[nme][lo:lo + rr, :], in_=wf32[nme][lo:lo + rr, :])),
                      writes=[("wbf", nme, l, s_)])

    T.dma("gq", lambda e: e.dma_start(out=x_sb[:], in_=xT.rearrange("(c p) t -> p c t", p=P)[:, :, 0:TILE]),
          writes=[("x", c) for c in range(KC)])
    issue_casts(4)
    for i in range(6):
        T.dma("gq", (lambda e, i=i: e.dma_start(out=t1[:], in_=ebias_d[i, :, :])), writes=["t1"])
        T.op("act", (lambda e, i=i: e.activation(out=etab[i][:], in_=t1[:], func=AF.Exp)), reads=["t1"], writes=[("etab", i)])

    wctr = [0]

    def wnext(nme, l, unit):
        r, w = wnames[nme]
        slot = wctr[0] % NSLOT
        wctr[0] += 1
        row = l * r + unit * P
        half = (unit * P) // (r // 2) if r > 2048 else 0
        T.dma("sync", (lambda e: e.dma_start(out=wsl[slot][:, 0:w], in_=wbf[nme][row:row + P, :])),
              reads=[("wbf", nme, l, half)], writes=[("ws", slot)], slot=slot)
        return slot

    def wview(slot, nk):
        return wsl[slot][:, 0:nk * P].rearrange("p (k m) -> p k m", m=P)

    def cc(l, off, n=1):
        return cst[:, l * LW + off:l * LW + off + n]

    sqi = [0]

    pend = []

    def flush_stats():
        while pend:
            pend.pop(0)()

    def stats_acc(src_ap, src_keys, first, last, lhs=None, lhs_key="ones", bank=7, defer=False):
        i = sqi[0] % 3
        sqi[0] += 1
        T.op("act", lambda e: e.activation(out=sq[i][:], in_=src_ap, func=AF.Square), reads=src_keys, writes=[("sq", i)])
        lh = ones_bf if lhs is None else lhs

        def emit():
            T.op("pe", lambda e: e.matmul(ps[bank][:], lhsT=lh[:], rhs=sq[i][:], start=first, stop=last),
                 reads=[("sq", i), lhs_key], writes=[PS(bank)])
        if defer:
            pend.append(emit)
        else:
            emit()

    def rstd_from(bank, scale, bias):
        T.op("act", lambda e: e.activation(out=rstd[:], in_=ps[bank][:], func=AF.Sqrt, scale=scale, bias=bias),
             reads=[PS(bank)], writes=["rstd"])
        T.op("dve", lambda e: e.reciprocal(out=rstd[:], in_=rstd[:]), reads=["rstd"], writes=["rstd"])

    def prenorm(l, goff):
        for c in range(KC):
            stats_acc(x_sb[:, c, :], [("x", c)], c == 0, c == KC - 1)
        rstd_from(7, 1.0 / D, EPS)
        for c in range(KC):
            T.op("dve", (lambda e, c=c: e.scalar_tensor_tensor(out=xn[:, c, :], in0=x_sb[:, c, :], scalar=cc(l, goff + c),
                                                                 in1=rstd[:], op0=ALU.mult, op1=ALU.mult)),
                 reads=[("x", c), "rstd", "cst"], writes=[("xn", c)])

    def postnorm_resid(l, goff, half):
        if half:
            rstd_from(7, 4.0 / D, 4.0 * EPS)
        else:
            rstd_from(7, 1.0 / D, EPS)
        for m in range(KC):
            tb, tk = (t1, "t1") if m % 2 == 0 else (t2, "t2")
            T.op("dve", (lambda e, m=m, tb=tb: e.scalar_tensor_tensor(out=tb[:], in0=yo[:, m, :], scalar=cc(l, goff + m),
                                                                        in1=rstd[:], op0=ALU.mult, op1=ALU.mult)),
                 reads=[("yo", m), "rstd", "cst"], writes=[tk])
            T.op("pool", (lambda e, m=m, tb=tb: e.tensor_tensor(out=x_sb[:, m, :], in0=x_sb[:, m, :], in1=tb[:], op=ALU.add)),
                 reads=[("x", m), tk], writes=[("x", m)])

    def evac_y(m, bank, first, last):
        T.op("act", lambda e: e.activation(out=yo[:, m, :], in_=ps[bank][:], func=AF.Copy), reads=[PS(bank)], writes=[("yo", m)])
        stats_acc(ps[bank][:], [PS(bank)], first, last, defer=True)

    def ffn(l, which):
        sfx = "1" if which == 1 else "2"
        gpre, gpost = (0, 8) if which == 1 else (32, 40)
        prenorm(l, gpre)
        for f in range(NF):
            bg, bu = f % 2, 2 + f % 2
            for nme, bank in (("wg" + sfx, bg), ("wu" + sfx, bu)):
                slot = wnext(nme, l, f)
                wv = wview(slot, KC)

                if f == 0:
                    for k in range(KC):
                        T.op("pe", (lambda e, wv=wv, bank=bank, k=k: e.matmul(ps[bank][:], lhsT=wv[:, k, :], rhs=xn[:, k, :],
                                                                                start=(k == 0), stop=(k == KC - 1))),
                             reads=[("ws", slot), ("xn", k)], writes=[PS(bank)])
                    continue

                def mm(e, wv=wv, bank=bank):
                    for k in range(KC):
                        ins = e.matmul(ps[bank][:], lhsT=wv[:, k, :], rhs=xn[:, k, :], start=(k == 0), stop=(k == KC - 1))
                    return ins
                T.op("pe", mm, reads=[("ws", slot)] + [("xn", c) for c in range(KC)], writes=[PS(bank)])
            T.op("act", (lambda e, f=f, bg=bg: e.activation(out=sg[f % 2][:], in_=ps[bg][:], func=AF.Silu)),
                 reads=[PS(bg)], writes=[("sg", f % 2)])
            T.op("dve", (lambda e, f=f, bu=bu: e.tensor_tensor(out=h_sb[:, f, :], in0=ps[bu][:], in1=sg[f % 2][:], op=ALU.mult)),
                 reads=[PS(bu), ("sg", f % 2)], writes=[("h", f)])
        for m in range(KC):
            bank = 4 + m % 2
            s0 = wnext("wd" + sfx, l, 2 * m)
            s1 = wnext("wd" + sfx, l, 2 * m + 1)
            v0, v1 = wview(s0, 11), wview(s1, 11)

            def mm(e, v0=v0, v1=v1, bank=bank):
                for f in range(NF):
                    wv = v0 if f < 11 else v1
                    ins = e.matmul(ps[bank][:], lhsT=wv[:, f % 11, :], rhs=h_sb[:, f, :], start=(f == 0), stop=(f == NF - 1))
                return ins
            T.op("pe", mm, reads=[("ws", s0), ("ws", s1)] + [("h", f) for f in range(NF)], writes=[PS(bank)])
            flush_stats()
            evac_y(m, bank, m == 0, m == KC - 1)
        flush_stats()
        postnorm_resid(l, gpost, True)

    def inproj(l, t, next_x_from_input=False):
        t0 = t * TILE
        part = 0 if t < cfg.ntp else 1
        lt = t if part == 0 else t - cfg.ntp
        nlt = cfg.ntp if part == 0 else cfg.nts
        prenorm(l, 16)
        T.dma("gq", lambda e: e.dma_start(out=xres.rearrange("(c p) t -> p c t", p=P)[:, :, t0:t0 + TILE], in_=x_sb[:]),
              reads=[("x", c) for c in range(KC)], writes=[("xres", t)])
        if next_x_from_input and t + 1 < cfg.ntile:
            t1_ = (t + 1) * TILE
            T.dma("gq", lambda e: e.dma_start(out=x_sb[:], in_=xT.rearrange("(c p) t -> p c t", p=P)[:, :, t1_:t1_ + TILE]),
                  writes=[("x", c) for c in range(KC)])
        T.dma("gq", lambda e: e.dma_start(out=sg[0][:], in_=cs_d[0, :, t0:t0 + TILE]), writes=[("sg", 0)])
        T.dma("gq", lambda e: e.dma_start(out=sg[1][:], in_=cs_d[1, :, t0:t0 + TILE]), writes=[("sg", 1)])

        def proj(unit, bank):
            slot = wnext("win", l, unit)
            wv = wview(slot, KC)

            def mm(e):
                for k in range(KC):
                    ins = e.matmul(ps[bank][:], lhsT=wv[:, k, :], rhs=xn[:, k, :], start=(k == 0), stop=(k == KC - 1))
                return ins
            T.op("pe", mm, reads=[("ws", slot)] + [("xn", c) for c in range(KC)], writes=[PS(bank)])

        def roped_a(u_main, u_sw, par, goff, out_ap, out_keys):
            b0, b1 = 2 * par, 2 * par + 1
            proj(u_main, b0)
            proj(u_sw, b1)
            stats_acc(ps[b0][:], [PS(b0)], True, True, lhs=bones_bf, lhs_key="bones", bank=6, defer=True)
            return pend.pop()

        def roped_b(stat_emit, u_main, u_sw, par, goff, out_ap, out_keys):
            b0, b1 = 2 * par, 2 * par + 1
            stat_emit()
            rstd_from(6, 1.0 / HD, EPS)
            T.op("dve", lambda e: e.scalar_tensor_tensor(out=t1[:], in0=ps[b0][:], scalar=cc(l, goff), in1=sg[0][:],
                                                         op0=ALU.mult, op1=ALU.mult), reads=[PS(b0), ("sg", 0), "cst"], writes=["t1"])
            T.op("dve", lambda e: e.scalar_tensor_tensor(out=t2[:], in0=ps[b1][:], scalar=cc(l, goff + 1), in1=sg[1][:],
                                                         op0=ALU.mult, op1=ALU.mult), reads=[PS(b1), ("sg", 1), "cst"], writes=["t2"])
            T.op("dve", lambda e: e.tensor_tensor(out=t1[:], in0=t1[:], in1=t2[:], op=ALU.add), reads=["t1", "t2"], writes=["t1"])
            T.op("dve", lambda e: e.tensor_tensor(out=out_ap, in0=t1[:], in1=rstd[:], op=ALU.mult), reads=["t1", "rstd"], writes=out_keys)

        qsc = q_scr[l].rearrange("(c p) t -> p c t", p=P)

        def qstore(c):
            i = c % 2
            T.dma("gq", lambda e: e.dma_start(out=qsc[:, c, t0:t0 + TILE], in_=qst[i][:]), reads=[("qst", i)], writes=[("q_scr", l, t, c)])

        rch = [(j, 4 + j, j % 3, 48, qst[j % 2][:], [("qst", j % 2)]) for j in range(4)] + [(8, 9, 4 % 3, 50, ksa[:], ["ksa"])]
        flush_stats()
        semit = {}
        for i in range(min(2, len(rch))):
            semit[i] = roped_a(*rch[i])
        for i in range(len(rch)):
            if i + 2 < len(rch):
                semit[i + 2] = roped_a(*rch[i + 2])
            roped_b(semit[i], *rch[i])
            if i < 4:
                qstore(i)
        for j in range(4):
            bank = j % 4
            proj(10 + j, bank)
            T.op("act", (lambda e, j=j, bank=bank: e.activation(out=qst[j % 2][:], in_=ps[bank][:], func=AF.Copy)),
                 reads=[PS(bank)], writes=[("qst", j % 2)])
            qstore(4 + j)
        proj(14, 0)
        T.op("act", lambda e: e.activation(out=ksb[:], in_=ps[0][:], func=AF.Copy), reads=[PS(0)], writes=["ksb"])
        sva = wnext("win", l, 15)
        svb = wnext("win", l, 16)
        wva, wvb = wview(sva, KC), wview(svb, KC)
        for blk in range(4):
            bank = 4 + blk % 2

            def mm(e, blk=blk, bank=bank):
                for wv, c0 in ((wva, 0), (wvb, P)):
                    for k in range(KC):
                        ins = e.matmul(ps[bank][:, c0:c0 + P], lhsT=xn[:, k, blk * P:(blk + 1) * P], rhs=wv[:, k, :],
                                       start=(k == 0), stop=(k == KC - 1))
                return ins
            T.op("pe", mm, reads=[("ws", sva), ("ws", svb)] + [("xn", c) for c in range(KC)], writes=[PS(bank)])
            for (dst, dkey, c0) in ((vstA, "vstA", 0), (vstB, "vstB", P)):
                for kv in range(2):
                    T.op("act", (lambda e, dst=dst, c0=c0, kv=kv, blk=blk, bank=bank: e.activation(
                        out=dst[:, blk, kv * 128:kv * 128 + 64], in_=ps[bank][:, c0 + kv * 64:c0 + kv * 64 + 64], func=AF.Copy)),
                        reads=[PS(bank)], writes=[dkey])
        gi, gs = t // 3, t % 3
        kin_ap = kin[l][gi].ap()
        q_written.add((l, t))
        T.dma("gq", lambda e: e.dma_start(out=kin_ap[:, gs * TSW:gs * TSW + TILE], in_=ksa[:]), reads=["ksa"], writes=[("kin", l, gi, gs, 0)])
        T.dma("gq", lambda e: e.dma_start(out=kin_ap[:, gs * TSW + TILE:(gs + 1) * TSW].rearrange("p (b c) -> p b c", c=192), in_=vstA[:]),
              reads=["vstA"], writes=[("kin", l, gi, gs, 1)])
        T.dma("gq", lambda e: e.dma_start(out=kvb_k[l][:, t0:t0 + TILE], in_=ksb[:]), reads=["ksb"], writes=[("kvbk", l, t)])
        T.dma("gq", lambda e: e.dma_start(out=kvb_v[l][:, t * 768:(t + 1) * 768].rearrange("p (b c) -> p b c", c=192), in_=vstB[:]),
              reads=["vstB"], writes=[("kvbv", l, t)])
        kb_ap = kinb[l][part].ap()
        if lt == 0:
            T.dma("gq", lambda e: e.dma_start(out=kb_ap[:, 0:128], in_=ksb[:, 0:128]), reads=["ksb"], writes=[("kinb", l, part, 0)])
            T.dma("gq", lambda e: e.dma_start(out=kb_ap[:, 256:448], in_=vstB[:, 0, :]), reads=["vstB"], writes=[("kinb", l, part, 2)])
        if lt == nlt - 1:
            T.dma("gq", lambda e: e.dma_start(out=kb_ap[:, 128:256], in_=ksb[:, 384:512]), reads=["ksb"], writes=[("kinb", l, part, 1)])
            T.dma("gq", lambda e: e.dma_start(out=kb_ap[:, 448:640], in_=vstB[:, 3, :]), reads=["vstB"], writes=[("kinb", l, part, 3)])
            deferred_ags.append(lambda: T.ag(lambda e: e.collective_compute("AllGather", ALU.bypass, replica_groups=GROUPS,
                                                                             ins=[kinb[l][part].ap().opt()], outs=[kallb[l][part].ap().opt()]),
                                             reads=[("kinb", l, part, i) for i in range(4)], writes=[("kallb", l, part)]))
        if gs == 2 or t == cfg.ntile - 1:
            def do_ag():
                ag_emitted.add((l, gi))
                T.ag(lambda e: e.collective_compute("AllGather", ALU.bypass, replica_groups=GROUPS,
                                                    ins=[kin[l][gi].ap().opt()], outs=[kall[l][gi].ap().opt()]),
                     reads=[("kin", l, gi, s_, i) for s_ in range(gs + 1) for i in range(2)], writes=[("kall", l, gi)])
            deferred_ags.append(do_ag)

    def fixup(a_num, a_rs_key, out_lo, out_hi, out_keys, sink):
        fixup1(a_num, a_rs_key, out_lo, out_hi, out_keys, sink)
        fixup2(a_num, a_rs_key, out_lo, out_hi, out_keys, sink)

    def fixup1(a_num, a_rs_key, out_lo, out_hi, out_keys, sink):
        if sink:
            T.op("dve", lambda e: e.tensor_tensor(out=Rt[64:128, :], in0=ps[4][64:128, :], in1=sinkx[64:128, :], op=ALU.add),
                 reads=[PS(4), "sinkx"], writes=["t1"])
            T.op("dve", lambda e: e.tensor_tensor(out=Rt[0:64, :], in0=ps[5][0:64, :], in1=sinkx[0:64, :], op=ALU.add),
                 reads=[PS(5), "sinkx"], writes=["t1"])
        else:
            T.op("dve", lambda e: e.tensor_copy(out=Rt[64:128, :], in_=ps[4][64:128, :]), reads=[PS(4)], writes=["t1"])
            T.op("dve", lambda e: e.tensor_copy(out=Rt[0:64, :], in_=ps[5][0:64, :]), reads=[PS(5)], writes=["t1"])
        T.op("act", lambda e: e.activation(out=out_lo, in_=a_num(ps[4][0:64, :]), func=AF.Copy), reads=[PS(4)], writes=out_keys)
        T.op("act", lambda e: e.activation(out=out_hi, in_=a_num(ps[5][64:128, :]), func=AF.Copy), reads=[PS(5)], writes=out_keys)
        T.op("dve", lambda e: e.reciprocal(out=Rt[:], in_=Rt[:]), reads=["t1"], writes=["t1"])

    def fixup2(a_num, a_rs_key, out_lo, out_hi, out_keys, sink):
        T.op("pe", lambda e: e.matmul(ps[6][:], lhsT=swp[:], rhs=Rt[:], start=True, stop=True), reads=["t1", "swp"], writes=[PS(6)])
        T.op("act", lambda e: e.activation(out=Ct[:], in_=ps[6][:], func=AF.Copy), reads=[PS(6)], writes=["t2"])
        T.op("dve", lambda e: e.tensor_tensor(out=out_lo, in0=out_lo, in1=a_num(Ct[0:64, :]), op=ALU.mult),
             reads=list(out_keys) + ["t2"], writes=out_keys)
        T.op("dve", lambda e: e.tensor_tensor(out=out_hi, in0=out_hi, in1=a_num(Ct[64:128, :]), op=ALU.mult),
             reads=list(out_keys) + ["t2"], writes=out_keys)

    kv_loaded = set()
    ag_emitted = set()
    deferred_ags = []

    def flush_ags():
        while deferred_ags:
            deferred_ags.pop(0)()

    def load_kv_resident(l, part):
        tiles = range(cfg.ntp) if part == 0 else range(cfg.ntp, cfg.ntile)
        if (l, part) in kv_loaded or not all((l, tt // 3) in ag_emitted for tt in tiles):
            return
        kv_loaded.add((l, part))
        tpart = cfg.TP if part == 0 else cfg.TS
        kview = KaT[:, 0:4 * tpart].rearrange("p (r n) -> p r n", r=4)
        vview = VaA[:, 0:4 * tpart // P, :].rearrange("p (r n) c -> p r (n c)", r=4)
        for tt in tiles:
            ltt = tt if part == 0 else tt - cfg.ntp
            gi, gs = tt // 3, tt % 3
            src = kall[l][gi].ap().rearrange("(r p) w -> p r w", p=P)
            T.dma("gq", (lambda e, src=src, gs=gs, ltt=ltt: e.dma_start(out=kview[:, :, ltt * TILE:(ltt + 1) * TILE],
                                                                        in_=src[:, :, gs * TSW:gs * TSW + TILE])),
                  reads=[("kall", l, gi)], writes=["KaT"])
            T.dma("gq", (lambda e, src=src, gs=gs, ltt=ltt: e.dma_start(out=vview[:, :, ltt * 768:(ltt + 1) * 768],
                                                                        in_=src[:, :, gs * TSW + TILE:(gs + 1) * TSW])),
                  reads=[("kall", l, gi)], writes=["VaA"])

    def halo(l, part, left):
        src = kallb[l][part].ap().rearrange("(r p) w -> p r w", p=P)
        kc0, vc0 = (128, 448) if left else (0, 256)
        slot = 0 if left else 5
        so = L * LW + (0 if left else 4)
        T.dma("gq", lambda e: e.dma_start(out=hlk[:], in_=src[:, :, kc0:kc0 + 128]), reads=[("kallb", l, part)], writes=["hlk"])
        T.dma("gq", lambda e: e.dma_start(out=hlv[:], in_=src[:, :, vc0:vc0 + 192]), reads=[("kallb", l, part)], writes=["hlv"])
        for (srct, skey, dst, dkey, w) in ((hlk, "hlk", kbt, "kbt", 128), (hlv, "hlv", vbt, "vbt", 192)):
            T.op("dve", (lambda e, srct=srct, w=w: e.tensor_scalar(out=hacc[:, 0:w], in0=srct[:, 0, :], scalar1=cst[:, so:so + 1],
                                                                    scalar2=None, op0=ALU.mult)),
                 reads=[skey, "cst"], writes=["hacc"])
            for r_ in range(1, 4):
                o_ap = hacc[:, 0:w] if r_ < 3 else dst[:, slot, :]
                T.op("dve", (lambda e, srct=srct, w=w, r_=r_, o_ap=o_ap: e.scalar_tensor_tensor(
                    out=o_ap, in0=srct[:, r_, :], scalar=cst[:, so + r_:so + r_ + 1], in1=hacc[:, 0:w], op0=ALU.mult, op1=ALU.add)),
                    reads=[skey, "cst", "hacc"], writes=["hacc"] if r_ < 3 else [dkey])

    q_written = set()
    q_loaded = [None]

    def load_q(l, t):
        if q_loaded[0] == (l, t) or (l, t) not in q_written:
            return
        q_loaded[0] = (l, t)
        tq = t * TILE
        T.dma("gq", lambda e: e.dma_start(out=qbuf[:], in_=q_scr[l].rearrange("(c p) t -> p c t", p=P)[:, :, tq:tq + TILE]),
              reads=[("q_scr", l, t, c) for c in range(KC)], writes=[("q", c) for c in range(KC)])

    def attn(l, t):
        t0 = t * TILE
        part = 0 if t < cfg.ntp else 1
        lt = t if part == 0 else t - cfg.ntp
        nlt = cfg.ntp if part == 0 else cfg.nts
        seq = cfg.seq_p if part == 0 else cfg.seq_s
        NK = seq // P
        if lt == 0:
            flush_ags()
            load_kv_resident(l, part)
            assert (l, part) in kv_loaded
        load_q(l, t)
        assert q_loaded[0] == (l, t)
        T.dma("gq", lambda e: e.dma_start(out=x_sb[:], in_=xres.rearrange("(c p) t -> p c t", p=P)[:, :, t0:t0 + TILE]),
              reads=[("xres", t)], writes=[("x", c) for c in range(KC)])
        T.dma("gq", lambda e: e.dma_start(out=kbt[:, 1:5, :], in_=kvb_k[l][:, t0:t0 + TILE].rearrange("p (b c) -> p b c", c=P)),
              reads=[("kvbk", l, t)], writes=["kbt"])
        T.dma("gq", lambda e: e.dma_start(out=vbt[:, 1:5, :], in_=kvb_v[l][:, t * 768:(t + 1) * 768].rearrange("p (b c) -> p b c", c=192)),
              reads=[("kvbv", l, t)], writes=["vbt"])
        if lt > 0:
            T.dma("gq", lambda e: e.dma_start(out=kbt[:, 0, :], in_=kvb_k[l][:, t0 - P:t0]), reads=[("kvbk", l, t - 1)], writes=["kbt"])
            T.dma("gq", lambda e: e.dma_start(out=vbt[:, 0, :], in_=kvb_v[l][:, t * 768 - 192:t * 768]), reads=[("kvbv", l, t - 1)], writes=["vbt"])
        else:
            halo(l, part, True)
        if lt < nlt - 1:
            T.dma("gq", lambda e: e.dma_start(out=kbt[:, 5, :], in_=kvb_k[l][:, t0 + TILE:t0 + TILE + P]), reads=[("kvbk", l, t + 1)], writes=["kbt"])
            T.dma("gq", lambda e: e.dma_start(out=vbt[:, 5, :], in_=kvb_v[l][:, (t + 1) * 768:(t + 1) * 768 + 192]), reads=[("kvbv", l, t + 1)], writes=["vbt"])
        else:
            halo(l, part, False)

        if l == 0 and t == 0:
            issue_casts(8)
        if l == 0 and (t == 1 or cfg.ntile == 1):
            issue_casts(len(pending_casts))
        flush_ags()
        v4 = lambda ap: ap.rearrange("p (h q) -> p h q", h=4)

        def b_steps(qb):
            steps = [(kv, oi) for kv in range(2) for oi in range(3)]
            LA = 2
            for si in range(len(steps) + LA):
                if si < len(steps):
                    kv, oi = steps[si]
                    lo = kv * 64
                    eidx = kv * 3 + oi
                    ks = qb + oi
                    sbank = si % 4
                    T.op("pe", (lambda e, sbank=sbank, lo=lo, ks=ks: e.matmul(
                        ps[sbank][:], lhsT=kbt[lo:lo + 64, ks, :], rhs=qbuf[lo:lo + 64, 4:8, qb * P:(qb + 1) * P], start=True, stop=True)),
                        reads=["kbt"] + [("q", 4 + jj) for jj in range(4)], writes=[PS(sbank)])
                    T.op("act", (lambda e, sbank=sbank: e.activation(out=sg[sbank % 2][:], in_=ps[sbank][:], func=AF.Exp, scale=0.125)),
                         reads=[PS(sbank)], writes=[("sg", sbank % 2)])
                    T.op("dve", (lambda e, sbank=sbank, eidx=eidx: e.tensor_tensor(out=pt[sbank][:], in0=sg[sbank % 2][:], in1=etab[eidx][:], op=ALU.mult)),
                         reads=[("sg", sbank % 2), ("etab", eidx)], writes=[("pt", sbank)])
                if si >= LA:
                    kv, oi = steps[si - LA]
                    ks = qb + oi
                    sbank = (si - LA) % 4
                    T.op("pe", (lambda e, sbank=sbank, kv=kv, ks=ks, oi=oi: e.matmul(
                        ps[4 + kv][:], lhsT=vbt[:, ks, kv * 64:kv * 64 + 128], rhs=pt[sbank][:], start=(oi == 0), stop=(oi == 2))),
                        reads=["vbt", ("pt", sbank)], writes=[PS(4 + kv)])

        def b_args(qb):
            return (v4, None, yo[0:64, 4:8, qb * P:(qb + 1) * P], yo[64:128, 4:8, qb * P:(qb + 1) * P],
                    [("yo", 4 + jj) for jj in range(4)], True)

        ident = lambda ap: ap
        pend_b2 = []
        for j in range(4):
            for kc in range(NK + 1):
                if kc == min(8, NK) and pend_b2:
                    fixup2(*pend_b2.pop())
                if kc < NK:
                    for hh in range(2):
                        b = (kc % 2) * 2 + hh
                        lo = hh * 64
                        T.op("pe", (lambda e, b=b, lo=lo, kc=kc, j=j: e.matmul(ps[b][:], lhsT=KaT[lo:lo + 64, kc * P:(kc + 1) * P],
                                                                                 rhs=qbuf[lo:lo + 64, j, :], start=True, stop=True)),
                             reads=["KaT", ("q", j)], writes=[PS(b)])
                    for hh in range(2):
                        b = (kc % 2) * 2 + hh
                        T.op("act", (lambda e, b=b: e.activation(out=pt[b][:], in_=ps[b][:], func=AF.Exp, scale=0.125)),
                             reads=[PS(b)], writes=[("pt", b)])
                if kc >= 1:
                    k1 = kc - 1
                    for hh in range(2):
                        b = (k1 % 2) * 2 + hh
                        T.op("pe", (lambda e, b=b, hh=hh, k1=k1: e.matmul(ps[4 + hh][:], lhsT=VaA[:, k1, hh * 64:hh * 64 + 128], rhs=pt[b][:],
                                                                            start=(k1 == 0), stop=(k1 == NK - 1))),
                             reads=["VaA", ("pt", b)], writes=[PS(4 + hh)])
            a_args = (ident, None, yo[0:64, j, :], yo[64:128, j, :], [("yo", j)], False)
            fixup1(*a_args)
            b_steps(j)
            fixup2(*a_args)
            stats_acc(yo[:, j, :], [("yo", j)], j == 0, j == 3)
            fixup1(*b_args(j))
            pend_b2.append(b_args(j))
        while pend_b2:
            fixup2(*pend_b2.pop())
        if lt == nlt - 1:
            if part == 0:
                load_kv_resident(l, 1)
            elif l + 1 < L:
                load_kv_resident(l + 1, 0)
        rstd_from(7, 1.0 / 512, EPS)
        for j in range(4):
            T.op("dve", (lambda e, j=j: e.scalar_tensor_tensor(out=xn[:, j, :], in0=yo[:, j, :], scalar=cc(l, 52 + j), in1=rstd[:],
                                                                 op0=ALU.mult, op1=ALU.mult)),
                 reads=[("yo", j), "rstd", "cst"], writes=[("xn", j)])

        if t + 1 < cfg.ntile:
            load_q(l, t + 1)
        elif l + 1 < L:
            load_q(l + 1, 0)
        for j in range(4):
            stats_acc(yo[:, 4 + j, :], [("yo", 4 + j)], j == 0, j == 3)
        rstd_from(7, 1.0 / 512, EPS)
        for j in range(4):
            T.op("dve", (lambda e, j=j: e.scalar_tensor_tensor(out=xn[:, 4 + j, :], in0=yo[:, 4 + j, :], scalar=cc(l, 56 + j), in1=rstd[:],
                                                                 op0=ALU.mult, op1=ALU.mult)),
                 reads=[("yo", 4 + j), "rstd", "cst"], writes=[("xn", 4 + j)])

        for m in range(KC):
            bank = 4 + m % 2
            slot = wnext("wout", l, m)
            wv = wview(slot, KC)

            def mm(e, wv=wv, bank=bank):
                for k in range(KC):
                    ins = e.matmul(ps[bank][:], lhsT=wv[:, k, :], rhs=xn[:, k, :], start=(k == 0), stop=(k == KC - 1))
                return ins
            T.op("pe", mm, reads=[("ws", slot)] + [("xn", c) for c in range(KC)], writes=[PS(bank)])
            flush_stats()
            evac_y(m, bank, m == 0, m == KC - 1)
        flush_stats()
        postnorm_resid(l, 24, False)

    def layer_consts(l):
        T.op("act", lambda e: e.activation(out=sx4[:], in_=cc(l, 60, 4), func=AF.Exp), reads=["cst"], writes=["sx4"])
        T.op("pool", lambda e: e.memset(sinkx[:], 0.0), writes=["sinkx"])
        for j in range(4):
            T.op("dve", (lambda e, j=j: e.tensor_scalar(out=sinkx[:, j * P:(j + 1) * P], in0=sinkx[:, j * P:(j + 1) * P],
                                                         scalar1=sx4[:, j:j + 1], scalar2=None, op0=ALU.add)),
                 reads=["sx4", "sinkx"], writes=["sinkx"])

    ncast_per_tile = (len(pending_casts) + cfg.ntile - 1) // cfg.ntile
    for t in range(cfg.ntile):
        t0 = t * TILE
        flush_ags()
        ffn(0, 1)
        inproj(0, t, next_x_from_input=True)
    for l in range(L):
        layer_consts(l)
        for t in range(cfg.ntile):
            t0 = t * TILE
            attn(l, t)
            ffn(l, 2)
            if l + 1 < L:
                ffn(l + 1, 1)
                inproj(l + 1, t)
            else:
                T.dma("gq", (lambda e, t0=t0: e.dma_start(out=yT.rearrange("(c p) t -> p c t", p=P)[:, :, t0:t0 + TILE], in_=x_sb[:])),
                      reads=[("x", c) for c in range(KC)], writes=[("yT", t)])
    T.finish()

    sems = {n: es.enter_context(nc.semaphore(n)) for n in T.sem_names()}
    with nc.Block() as block:
        @block.tensor
        def _(e):
            replay(T.q["pe"], e, sems)

        @block.scalar
        def _(e):
            replay(T.q["act"], e, sems)

        @block.vector
        def _(e):
            replay(T.q["dve"], e, sems)

        @block.gpsimd
        def _(e):
            replay(T.q["pool"], e, sems)

        @block.sync
        def _(e):
            replay(T.q["sync"], e, sems)
    es.close()
    return nc, T


def _units_k1024(w, cols_list):
    out = []
    for cols in cols_list:
        u = w[:, cols].reshape(KC, P, P).transpose(1, 0, 2).reshape(P, KC * P)
        out.append(u)
    return np.concatenate(out, axis=0)


def _prep_weights(inp, L):
    f32 = np.float32
    permd = np.concatenate([np.arange(0, 64, 2), np.arange(1, 64, 2)])
    swd = permd[(np.arange(64) + 32) % 64]
    ar = np.arange(64)
    res = {k: [] for k in ("wg1", "wu1", "wd1", "win", "wout", "wg2", "wu2", "wd2")}
    for l in range(L):
        ffnw = {"1": (inp["ffn1_w_gate"], inp["ffn1_w_up"], inp["ffn1_w_down"]),
                "2": (inp["ffn2_w_gate"], inp["ffn2_w_up"], inp["ffn2_w_down"])}
        for sfx in ("1", "2"):
            wg = np.asarray(ffnw[sfx][0][l], f32)
            wu = np.asarray(ffnw[sfx][1][l], f32)
            wd = np.asarray(ffnw[sfx][2][l], f32)
            cl = [np.arange(f * P, (f + 1) * P) for f in range(NF)]
            res["wg" + sfx].append(_units_k1024(wg, cl))
            res["wu" + sfx].append(_units_k1024(wu, cl))
            units = []
            for m in range(KC):
                for hf in range(2):
                    blk = wd[hf * 11 * P:(hf + 1) * 11 * P, m * P:(m + 1) * P]
                    units.append(blk.reshape(11, P, P).transpose(1, 0, 2).reshape(P, 11 * P))
            res["wd" + sfx].append(np.concatenate(units, axis=0))
        w_in = np.asarray(inp["w_in"][l], f32)
        qa, ka, va, qb, kb, vb = 0, 512, 640, 768, 1280, 1408
        cl = []
        for j in range(4):
            cl.append(np.concatenate([qa + j * 64 + permd, qa + (4 + j) * 64 + permd]))
        for j in range(4):
            cl.append(np.concatenate([qa + j * 64 + swd, qa + (4 + j) * 64 + swd]))
        cl.append(np.concatenate([ka + permd, ka + 64 + permd]))
        cl.append(np.concatenate([ka + swd, ka + 64 + swd]))
        for j in range(4):
            cl.append(np.concatenate([qb + j * 64 + ar, qb + (4 + j) * 64 + ar]))
        cl.append(kb + np.arange(P))
        cl.append(va + np.arange(P))
        cl.append(vb + np.arange(P))
        res["win"].append(_units_k1024(w_in, cl))
        w_out = np.asarray(inp["w_out"][l], f32)
        rows = []
        for c in range(8):
            base = 0 if c < 4 else 512
            j = c % 4
            rows.append(np.concatenate([base + j * 64 + ar, base + (4 + j) * 64 + ar]))
        rows = np.concatenate(rows)
        wo = w_out[rows, :]
        units = []
        for m in range(KC):
            units.append(wo[:, m * P:(m + 1) * P].reshape(KC, P, P).transpose(1, 0, 2).reshape(P, KC * P))
        res["wout"].append(np.concatenate(units, axis=0))
    return {k: np.ascontiguousarray(np.concatenate(v, axis=0)) for k, v in res.items()}, permd, swd


def _prep_consts(inp, cfg, permd, swd):
    L = cfg.depth
    f32 = np.float32
    base = np.zeros((P, cfg.ncst), f32)
    pidx = np.arange(P)

    def fm(v):
        return np.asarray(v, f32).reshape(KC, P).T

    for l in range(L):
        o = l * LW
        base[:, o + 0:o + 8] = fm(inp["ffn1_pre"][l])
        base[:, o + 8:o + 16] = fm(inp["ffn1_post"][l])
        base[:, o + 16:o + 24] = fm(inp["mix_pre"][l])
        base[:, o + 24:o + 32] = fm(inp["mix_post"][l])
        base[:, o + 32:o + 40] = fm(inp["ffn2_pre"][l])
        base[:, o + 40:o + 48] = fm(inp["ffn2_post"][l])
        gq = np.asarray(inp["a_q_norm"][l], f32)
        gk = np.asarray(inp["a_k_norm"][l], f32)
        base[:, o + 48] = gq[permd[pidx % 64]]
        base[:, o + 49] = gq[swd[pidx % 64]]
        base[:, o + 50] = gk[permd[pidx % 64]]
        base[:, o + 51] = gk[swd[pidx % 64]]
        ga = np.asarray(inp["a_out_norm"][l], f32)
        gb = np.asarray(inp["b_out_norm"][l], f32)
        sk = np.asarray(inp["b_sink"][l], f32)
        for j in range(4):
            hsel = np.where(pidx < 64, j, 4 + j)
            base[:, o + 52 + j] = ga[hsel * 64 + pidx % 64]
            base[:, o + 56 + j] = gb[hsel * 64 + pidx % 64]
            base[:, o + 60 + j] = np.where(pidx >= 64, sk[j], sk[4 + j])
    return base


def _rope_tables(cfg, r):
    f32 = np.float32
    inv_freq = (np.float32(10000.0) ** (-np.arange(16, dtype=f32) / np.float32(16))).astype(f32)
    pos = np.concatenate([r * cfg.TP + np.arange(cfg.TP), r * cfg.TS + np.arange(cfg.TS)])
    row = (pos // 64).astype(f32)
    col = (pos % 64).astype(f32)
    ang = np.concatenate([row[:, None] * inv_freq[None, :], col[:, None] * inv_freq[None, :]], axis=1).astype(f32)
    c, s = np.cos(ang).astype(f32), np.sin(ang).astype(f32)
    pidx = np.arange(P)
    pair = (pidx % 64) % 32
    sign = np.where((pidx % 64) < 32, -1.0, 1.0).astype(f32)
    cs = np.stack([c[:, pair].T, s[:, pair].T * sign[:, None]], axis=0)
    return np.ascontiguousarray(cs.astype(f32))


def _ebias(r):
    f32 = np.float32
    out = np.zeros((6, P, TILE), f32)
    k = np.arange(P)[:, None]
    q = np.arange(P)[None, :]
    for kv in range(2):
        for var in range(3):
            off = var - 1
            dist = np.abs(q - k - 128 * off).astype(f32)
            for j in range(4):
                h = kv * 4 + j
                slope = np.float32(2.0 ** (-(h + 1)))
                b = np.where(dist <= 128, -slope * dist, -30000.0).astype(f32)
                out[kv * 3 + var, :, j * P:(j + 1) * P] = b
    return out


_CACHE = {}


def kernel(**inp):
    xp = np.asarray(inp["x_prompt"], np.float32)
    xs = np.asarray(inp["x_sample"], np.float32)
    L = int(np.asarray(inp["w_in"]).shape[0])
    cfg = Cfg(xp.shape[1], xs.shape[1], L)
    key = (cfg.seq_p, cfg.seq_s, L)
    if key not in _CACHE:
        _CACHE[key] = build(cfg)[0]
    nc = _CACHE[key]
    wts, permd, swd = _prep_weights(inp, L)
    cbase = _prep_consts(inp, cfg, permd, swd)
    swpm = np.zeros((P, P), np.float32)
    swpm[np.arange(P), (np.arange(P) + 64) % P] = 1.0
    in_maps = []
    for c in range(8):
        g, r = c // 4, c % 4
        xt = np.concatenate([xp[g, r * cfg.TP:(r + 1) * cfg.TP, :], xs[g, r * cfg.TS:(r + 1) * cfg.TS, :]], axis=0)
        cstc = cbase.copy()
        so = L * LW
        if r > 0:
            cstc[:, so + r - 1] = 1.0
        if r < 3:
            cstc[:, so + 4 + r + 1] = 1.0
        m = {"xT": np.ascontiguousarray(xt.T), "cs": _rope_tables(cfg, r), "cst": cstc, "ebias": _ebias(r), "swp": swpm}
        m.update(wts)
        in_maps.append(m)
    res = run_bass_kernel_spmd(nc, in_maps, core_ids=list(range(8)))
    yp = np.zeros_like(xp)
    ys = np.zeros_like(xs)
    for c in range(8):
        g, r = c // 4, c % 4
        y = np.asarray(res.results[c]["yT"], np.float32).T
        yp[g, r * cfg.TP:(r + 1) * cfg.TP, :] = y[:cfg.TP]
        ys[g, r * cfg.TS:(r + 1) * cfg.TS, :] = y[cfg.TP:]
    return (yp, ys)
```
